# Optimizing a Trainium2 kernel written in Bass

```python
import math
import jax, jax.numpy as jnp
from jax import lax
import numpy as np

D_MODEL = 1024
BATCH = 1
SEQ = 16384
DEPTH = 2

NORM_EPS = 1e-6
CONV_K = 4
ATTN_HEADS = 8
ATTN_HEAD_DIM = 64
ATTN_WIDTH = ATTN_HEADS * ATTN_HEAD_DIM
DILATED_PATTERNS = ((128, 1), (512, 4), (2048, 16))
ATTN_BLOCK = 128
LRU_WIDTH = D_MODEL // 2
LRU_BLOCKS = 8
LRU_C = 8.0
AB_IN = 3 * ATTN_WIDTH + 2 * LRU_WIDTH
DN_HEADS = 8
DN_HEAD_DIM = 128
DN_WIDTH = DN_HEADS * DN_HEAD_DIM
DN_CHUNK = 64
DN_IN = 4 * DN_WIDTH + 2 * DN_HEADS
XA_HEADS = 4
XA_HEAD_DIM = D_MODEL // XA_HEADS
N_MEM = 256
D_FF = ((8 * D_MODEL // 3 + 127) // 128) * 128

kernel_name = "hybrid_dilated_attn_rglru_gdn_macaron"


def rmsnorm(x, g):
    xf = x.astype(jnp.float32)
    y = xf * lax.rsqrt(jnp.mean(xf * xf, axis=-1, keepdims=True) + NORM_EPS)
    return (y * g.astype(jnp.float32)).astype(x.dtype)


def swiglu(h, w_in, w_out):
    gate, up = jnp.split(h @ w_in, 2, axis=-1)
    return (jax.nn.silu(gate) * up) @ w_out


def causal_dwconv(x, w):
    C = x.shape[-1]
    return lax.conv_general_dilated(
        x, w[:, None, :].astype(x.dtype), window_strides=(1,),
        padding=((w.shape[0] - 1, 0),), dimension_numbers=('NWC', 'WIO', 'NWC'),
        feature_group_count=C)


def dilated_window_attention(q, k, v, dilation, n_back):
    B, S, H, hd = q.shape
    L = S // dilation
    Bd = B * dilation
    Lp = -(-L // ATTN_BLOCK) * ATTN_BLOCK
    nb = Lp // ATTN_BLOCK

    def split(t):
        t = t.reshape(B, L, dilation, H, hd).transpose(0, 2, 1, 3, 4).reshape(Bd, L, H, hd)
        return jnp.pad(t, ((0, 0), (0, Lp - L), (0, 0), (0, 0))).reshape(Bd, nb, ATTN_BLOCK, H, hd)

    qb, kb, vb = split(q), split(k), split(v)

    def with_prev(t):
        prev = jnp.concatenate([jnp.zeros_like(t[:, :1]), t[:, :-1]], axis=1)
        return jnp.concatenate([prev, t], axis=2)

    kk, vv = with_prev(kb), with_prev(vb)
    s = jnp.einsum('bnqhd,bnkhd->bnhqk', qb, kk).astype(jnp.float32) * (hd ** -0.5)
    qi = jnp.arange(ATTN_BLOCK)[:, None]
    kj = jnp.arange(2 * ATTN_BLOCK)[None, :]
    dist = qi + ATTN_BLOCK - kj
    band = (dist >= 0) & (dist <= n_back)
    not_first = (jnp.arange(nb) > 0)[:, None, None] | (kj >= ATTN_BLOCK)[None]
    valid = band[None] & not_first
    s = jnp.where(valid[None, :, None], s, -jnp.inf)
    m = jnp.max(s, axis=-1, keepdims=True)
    p = jnp.exp(s - m)
    den = jnp.sum(p, axis=-1)
    o = jnp.einsum('bnhqk,bnkhd->bnqhd', p, vv.astype(jnp.float32))
    den_t = jnp.moveaxis(den, 2, 3)
    o = o / den_t[..., None]
    lse = jnp.moveaxis(m[..., 0], 2, 3) + jnp.log(den_t)
    o = o.reshape(Bd, Lp, H, hd)[:, :L].reshape(B, dilation, L, H, hd).transpose(0, 2, 1, 3, 4)
    lse = lse.reshape(Bd, Lp, H)[:, :L].reshape(B, dilation, L, H).transpose(0, 2, 1, 3)
    return o.reshape(B, S, H, hd), lse.reshape(B, S, H)


def linear_scan(a, b):
    def comb(l, r):
        return (l[0] * r[0], r[0] * l[1] + r[1])
    _, h = lax.associative_scan(comb, (a, b), axis=1)
    return h


def attn_lru_mixer(h, w_in, conv_w, conv_b, w_a, b_a, w_x, b_x, lam, w_out):
    B, S, _ = h.shape
    A = ATTN_WIDTH
    q, k, v, xr, gr = jnp.split(h @ w_in, [A, 2 * A, 3 * A, 3 * A + LRU_WIDTH], axis=-1)
    shp = (B, S, ATTN_HEADS, ATTN_HEAD_DIM)
    q, k, v = q.reshape(shp), k.reshape(shp), v.reshape(shp)
    outs, lses = [], []
    for window, dil in DILATED_PATTERNS:
        o, l = dilated_window_attention(q, k, v, dil, window // dil)
        outs.append(o)
        lses.append(l)
    wts = jax.nn.softmax(jnp.stack(lses, 0), axis=0)
    attn = jnp.einsum('gbsh,gbshd->bshd', wts, jnp.stack(outs, 0))
    attn = attn.reshape(B, S, A).astype(h.dtype)
    xc = (causal_dwconv(xr, conv_w) + conv_b).astype(jnp.float32)
    xb = xc.reshape(B, S, LRU_BLOCKS, LRU_WIDTH // LRU_BLOCKS)
    r = jax.nn.sigmoid(jnp.einsum('bsnj,njk->bsnk', xb, w_a.astype(jnp.float32)).reshape(B, S, LRU_WIDTH)
                       + b_a.astype(jnp.float32))
    i = jax.nn.sigmoid(jnp.einsum('bsnj,njk->bsnk', xb, w_x.astype(jnp.float32)).reshape(B, S, LRU_WIDTH)
                       + b_x.astype(jnp.float32))
    log_a = -LRU_C * r * jax.nn.softplus(-lam.astype(jnp.float32))
    a = jnp.exp(log_a)
    mult = jnp.sqrt(-jnp.expm1(2.0 * log_a))
    hs = linear_scan(a, mult * i * xc)
    y = hs.astype(h.dtype) * jax.nn.gelu(gr)
    return jnp.concatenate([attn, y], axis=-1) @ w_out


def l2norm(t):
    return t * lax.rsqrt(jnp.sum(t * t, axis=-1, keepdims=True) + 1e-6)


def gated_delta_rule_chunked(q, k, v, g, beta):
    B, S, H, dk = q.shape
    dv = v.shape[-1]
    C = DN_CHUNK
    N = S // C

    def ch(t):
        t = jnp.moveaxis(t, 2, 1)
        return t.reshape((B, H, N, C) + t.shape[3:])

    q = ch(q * (dk ** -0.5))
    k, v, g, beta = ch(k), ch(v), ch(g), ch(beta)
    gcum = jnp.cumsum(g, axis=-1)
    tril = jnp.tril(jnp.ones((C, C), bool))
    strict = jnp.tril(jnp.ones((C, C), bool), -1)
    decay = jnp.exp(jnp.where(tril, gcum[..., :, None] - gcum[..., None, :], -jnp.inf))
    kb = k * beta[..., None]
    kkt = jnp.einsum('bhnid,bhnjd->bhnij', kb, k) * decay
    amat = jnp.eye(C, dtype=jnp.float32) + jnp.where(strict, kkt, 0.0)
    rhs = jnp.concatenate([v * beta[..., None], kb * jnp.exp(gcum)[..., None]], axis=-1)
    sol = lax.linalg.triangular_solve(amat, rhs, left_side=True, lower=True, unit_diagonal=True)
    u, w = sol[..., :dv], sol[..., dv:]
    qk = jnp.einsum('bhnid,bhnjd->bhnij', q, k) * decay

    def step(state, inp):
        q_i, k_i, u_i, w_i, qk_i, g_i = inp
        v_new = u_i - jnp.einsum('bhcd,bhde->bhce', w_i, state)
        o = (jnp.einsum('bhcd,bhde->bhce', q_i * jnp.exp(g_i)[..., None], state)
             + jnp.einsum('bhij,bhje->bhie', qk_i, v_new))
        g_last = g_i[..., -1]
        state = (state * jnp.exp(g_last)[..., None, None]
                 + jnp.einsum('bhcd,bhce->bhde', k_i * jnp.exp(g_last[..., None] - g_i)[..., None], v_new))
        return state, o

    xs = tuple(jnp.moveaxis(t, 2, 0) for t in (q, k, u, w, qk, gcum))
    state0 = jnp.zeros((B, H, dk, dv), jnp.float32)
    _, o = lax.scan(step, state0, xs)
    o = jnp.transpose(o, (1, 0, 3, 2, 4)).reshape(B, S, H, dv)
    return o


def deltanet_mixer(h, w_in, conv_w, a_log, dt_bias, o_norm, w_out):
    B, S, _ = h.shape
    W = DN_WIDTH
    qkv, z, a, b = jnp.split(h @ w_in, [3 * W, 4 * W, 4 * W + DN_HEADS], axis=-1)
    qkv = jax.nn.silu(causal_dwconv(qkv, conv_w)).astype(jnp.float32)
    q, k, v = jnp.split(qkv, 3, axis=-1)
    shp = (B, S, DN_HEADS, DN_HEAD_DIM)
    q, k, v = l2norm(q.reshape(shp)), l2norm(k.reshape(shp)), v.reshape(shp)
    beta = jax.nn.sigmoid(b.astype(jnp.float32))
    g = -jnp.exp(a_log.astype(jnp.float32)) * jax.nn.softplus(a.astype(jnp.float32) + dt_bias.astype(jnp.float32))
    o = gated_delta_rule_chunked(q, k, v, g, beta)
    o = rmsnorm(o, o_norm) * jax.nn.silu(z.reshape(shp).astype(jnp.float32))
    return o.reshape(B, S, W).astype(h.dtype) @ w_out


def cross_attention(h, mem_h, wq, wkv, wo):
    B, S, _ = h.shape
    M = mem_h.shape[1]
    q = (h @ wq).reshape(B, S, XA_HEADS, XA_HEAD_DIM)
    k, v = jnp.split(mem_h @ wkv, 2, axis=-1)
    k = k.reshape(B, M, XA_HEADS, XA_HEAD_DIM)
    v = v.reshape(B, M, XA_HEADS, XA_HEAD_DIM)
    s = jnp.einsum('bshd,bmhd->bhsm', q, k).astype(jnp.float32) * (XA_HEAD_DIM ** -0.5)
    p = jax.nn.softmax(s, axis=-1).astype(v.dtype)
    o = jnp.einsum('bhsm,bmhd->bshd', p, v).reshape(B, S, D_MODEL)
    return o @ wo


def setup_inputs(seed: int = 0) -> dict:
    key = jax.random.key(seed)
    keys = iter(jax.random.split(key, 64))
    n_even = (DEPTH + 1) // 2
    n_odd = DEPTH // 2
    f32 = jnp.float32

    def nrm(shape, fan_in):
        return jax.random.normal(next(keys), shape, f32) * (fan_in ** -0.5)

    def gain(shape):
        return 1.0 + 0.02 * jax.random.normal(next(keys), shape, f32)

    def small(shape):
        return 0.01 * jax.random.normal(next(keys), shape, f32)

    x = jax.random.normal(next(keys), (BATCH, SEQ, D_MODEL), f32)
    mem = jax.random.normal(next(keys), (BATCH, N_MEM, D_MODEL), f32)
    blk = LRU_WIDTH // LRU_BLOCKS
    a0 = jax.random.uniform(next(keys), (n_even, LRU_WIDTH), f32, 0.9, 0.999)
    a_base = a0 ** (1.0 / LRU_C)
    lam = jnp.log(a_base) - jnp.log1p(-a_base)
    dt = jnp.exp(jax.random.uniform(next(keys), (n_odd, DN_HEADS), f32, math.log(1e-3), math.log(0.1)))
    dt_bias = dt + jnp.log(-jnp.expm1(-dt))
    a_log = jnp.log(jax.random.uniform(next(keys), (n_odd, DN_HEADS), f32, 1.0, 16.0))
    return {
        "x": x,
        "mem": mem,
        "ffn1_norm": gain((DEPTH, D_MODEL)),
        "ffn1_w_in": nrm((DEPTH, D_MODEL, 2 * D_FF), D_MODEL),
        "ffn1_w_out": nrm((DEPTH, D_FF, D_MODEL), D_FF),
        "mix_norm": gain((DEPTH, D_MODEL)),
        "xa_norm": gain((DEPTH, D_MODEL)),
        "xa_mem_norm": gain((DEPTH, D_MODEL)),
        "xa_wq": nrm((DEPTH, D_MODEL, D_MODEL), D_MODEL),
        "xa_wkv": nrm((DEPTH, D_MODEL, 2 * D_MODEL), D_MODEL),
        "xa_wo": nrm((DEPTH, D_MODEL, D_MODEL), D_MODEL),
        "ffn2_norm": gain((DEPTH, D_MODEL)),
        "ffn2_w_in": nrm((DEPTH, D_MODEL, 2 * D_FF), D_MODEL),
        "ffn2_w_out": nrm((DEPTH, D_FF, D_MODEL), D_FF),
        "ab_w_in": nrm((n_even, D_MODEL, AB_IN), D_MODEL),
        "lru_conv_w": nrm((n_even, CONV_K, LRU_WIDTH), CONV_K),
        "lru_conv_b": small((n_even, LRU_WIDTH)),
        "lru_w_a": nrm((n_even, LRU_BLOCKS, blk, blk), blk),
        "lru_b_a": small((n_even, LRU_WIDTH)),
        "lru_w_x": nrm((n_even, LRU_BLOCKS, blk, blk), blk),
        "lru_b_x": small((n_even, LRU_WIDTH)),
        "lru_lambda": lam,
        "ab_w_out": nrm((n_even, ATTN_WIDTH + LRU_WIDTH, D_MODEL), ATTN_WIDTH + LRU_WIDTH),
        "dn_w_in": nrm((n_odd, D_MODEL, DN_IN), D_MODEL),
        "dn_conv_w": nrm((n_odd, CONV_K, 3 * DN_WIDTH), CONV_K),
        "dn_a_log": a_log,
        "dn_dt_bias": dt_bias,
        "dn_o_norm": gain((n_odd, DN_HEAD_DIM)),
        "dn_w_out": nrm((n_odd, DN_WIDTH, D_MODEL), DN_WIDTH),
        "final_norm": gain((D_MODEL,)),
    }


def reference(x, mem, ffn1_norm, ffn1_w_in, ffn1_w_out, mix_norm, xa_norm, xa_mem_norm,
              xa_wq, xa_wkv, xa_wo, ffn2_norm, ffn2_w_in, ffn2_w_out,
              ab_w_in, lru_conv_w, lru_conv_b, lru_w_a, lru_b_a, lru_w_x, lru_b_x, lru_lambda, ab_w_out,
              dn_w_in, dn_conv_w, dn_a_log, dn_dt_bias, dn_o_norm, dn_w_out, final_norm):
    for layer in range(DEPTH):
        x = x + 0.5 * swiglu(rmsnorm(x, ffn1_norm[layer]), ffn1_w_in[layer], ffn1_w_out[layer])
        h = rmsnorm(x, mix_norm[layer])
        j = layer // 2
        if layer % 2 == 0:
            x = x + attn_lru_mixer(h, ab_w_in[j], lru_conv_w[j], lru_conv_b[j], lru_w_a[j], lru_b_a[j],
                                   lru_w_x[j], lru_b_x[j], lru_lambda[j], ab_w_out[j])
        else:
            x = x + deltanet_mixer(h, dn_w_in[j], dn_conv_w[j], dn_a_log[j], dn_dt_bias[j],
                                   dn_o_norm[j], dn_w_out[j])
        x = x + cross_attention(rmsnorm(x, xa_norm[layer]), rmsnorm(mem, xa_mem_norm[layer]),
                                xa_wq[layer], xa_wkv[layer], xa_wo[layer])
        x = x + 0.5 * swiglu(rmsnorm(x, ffn2_norm[layer]), ffn2_w_in[layer], ffn2_w_out[layer])
    return rmsnorm(x, final_norm)
```

```python
import numpy as np
import ml_dtypes
import concourse.bass as bass
import concourse.mybir as mybir
from concourse.bass_utils import run_bass_kernel_spmd

F32 = mybir.dt.float32
BF16 = mybir.dt.bfloat16
AF = mybir.ActivationFunctionType
ALU = mybir.AluOpType
AX = mybir.AxisListType
NPBF = ml_dtypes.bfloat16

NCORE = 8
D = 1024
S = 16384
TPC = S // NCORE
DFF = 2816
NFF = DFF // 128
EPS = 1e-6
NMEM = 256
NEG = -30000.0

NDMA_SEMS = 8


class Buf:
    def __init__(self, t, name=""):
        self.t = t
        self.name = name
        self.last_write = None
        self.reads = []

    def __getitem__(self, idx):
        return self.t[idx]


class Prog:
    ENGS = ("pe", "act", "dve", "pool", "sp")

    def __init__(self):
        self.nc = bass.Bass("TRN2", target_bir_lowering=False)
        nc = self.nc
        self.ops = {e: [] for e in self.ENGS}
        self.sem = {e: nc.alloc_semaphore("s_" + e) for e in self.ENGS}
        self.count = {e: 0 for e in self.ENGS}
        self.dsem = {e: [nc.alloc_semaphore(f"d_{e}{i}") for i in range(NDMA_SEMS)] for e in ("sp", "act", "pool")}
        self.dcount = {e: 0 for e in ("sp", "act", "pool")}
        self.waited = {e: {} for e in self.ENGS}
        self.nbuf = 0
        self.outs = []

    def sbuf(self, shape, dtype=F32, name=None):
        self.nbuf += 1
        name = name or f"sb{self.nbuf}"
        return Buf(self.nc.alloc_sbuf_tensor(name, list(shape), dtype), name)

    def psum(self, shape, dtype=F32, name=None):
        self.nbuf += 1
        name = name or f"ps{self.nbuf}"
        return Buf(self.nc.alloc_psum_tensor(name, list(shape), dtype), name)

    def dram(self, name, shape, dtype=F32, kind="Internal"):
        b = Buf(self.nc.dram_tensor(name, list(shape), dtype, kind=kind), name)
        if kind == "ExternalOutput":
            self.outs.append(b)
        return b

    def _deps(self, e, reads, writes):
        deps = set()
        for b in reads:
            if b.last_write is not None:
                deps.add(b.last_write)
        for b in writes:
            if b.last_write is not None:
                deps.add(b.last_write)
            deps.update(b.reads)
        waits = []
        for key, val in deps:
            if key == e and e == "pe":
                continue
            if self.waited[e].get(key, 0) >= val:
                continue
            self.waited[e][key] = val
            waits.append((key, val))
        return waits

    def _semof(self, key):
        if isinstance(key, tuple):
            return self.dsem[key[1]][key[2]]
        return self.sem[key]

    def op(self, e, fn, reads=(), writes=()):
        waits = self._deps(e, reads, writes)
        self.count[e] += 1
        seq = self.count[e]
        sem = self.sem[e]
        wl = [(self._semof(k), v) for k, v in waits]

        def emit(eng):
            for s, v in wl:
                eng.wait_ge(s, v)
            fn(eng).then_inc(sem, 1)

        self.ops[e].append(emit)
        ev = (e, seq)
        for b in reads:
            b.reads.append(ev)
        for b in writes:
            b.last_write = ev
            b.reads = []
        return ev

    def dma(self, q, out_ap, in_ap, reads=(), writes=(), **kw):
        j = self.dcount[q]
        self.dcount[q] += 1
        slot = j % NDMA_SEMS
        rnd = j // NDMA_SEMS
        key = ("d", q, slot)
        waits = self._deps(q, reads, writes)
        if rnd > 0 and self.waited[q].get(key, 0) < 16 * rnd:
            self.waited[q][key] = 16 * rnd
            waits.append((key, 16 * rnd))
        wl = [(self._semof(k), v) for k, v in waits]
        sem = self.dsem[q][slot]

        def emit(eng):
            for s, v in wl:
                eng.wait_ge(s, v)
            eng.dma_start(out=out_ap, in_=in_ap, **kw).then_inc(sem, 16)

        self.ops[q].append(emit)
        ev = (key, 16 * (rnd + 1))
        for b in reads:
            b.reads.append(ev)
        for b in writes:
            b.last_write = ev
            b.reads = []
        return ev

    def build(self):
        waits = self._deps("sp", self.outs, ())
        wl = [(self._semof(k), v) for k, v in waits]
        self.ops["sp"].append(lambda eng: [eng.wait_ge(s, v) for s, v in wl])
        nc = self.nc
        ops = self.ops
        with nc.Block() as block:
            @block.tensor
            def _(eng):
                for f in ops["pe"]:
                    f(eng)

            @block.scalar
            def _(eng):
                for f in ops["act"]:
                    f(eng)

            @block.vector
            def _(eng):
                for f in ops["dve"]:
                    f(eng)

            @block.gpsimd
            def _(eng):
                for f in ops["pool"]:
                    f(eng)

            @block.sync
            def _(eng):
                for f in ops["sp"]:
                    f(eng)
        return nc


class Ring:
    def __init__(self, bufs):
        self.bufs = bufs
        self.i = 0

    def next(self):
        b = self.bufs[self.i % len(self.bufs)]
        self.i += 1
        return b


def vec128(v):
    v = np.asarray(v)
    return np.ascontiguousarray(v.reshape(-1, 128).T)


def run_prog(P, in_maps):
    P.build()
    res = run_bass_kernel_spmd(P.nc, in_maps, core_ids=list(range(NCORE)))
    return res.results


class TokCtx:
    def __init__(self, P, ntok=TPC):
        self.P = P
        self.ntok = ntok
        self.xT = [[P.sbuf([128, 512], F32, name=f"xT{c}_{t}") for t in range(ntok // 512)] for c in range(8)]
        self.ones = P.sbuf([128, 128], BF16, name="ones")
        P.op("dve", lambda e: e.memset(self.ones[:], 1.0), writes=[self.ones])
        self.hT = [P.sbuf([128, 512], BF16, name=f"hT{c}") for c in range(8)]
        self.sq = Ring([P.sbuf([128, 512], BF16, name=f"sq{i}") for i in range(3)])
        self.rstd = P.sbuf([128, 512], F32, name="rstd")
        self.ps_stat = P.psum([128, 512], F32, name="ps_stat")
        self.ps_a = Ring([P.psum([128, 512], F32, name=f"ps_a{i}") for i in range(2)])
        self.ps_b = Ring([P.psum([128, 512], F32, name=f"ps_b{i}") for i in range(2)])
        self.ps_o = Ring([P.psum([128, 512], F32, name=f"ps_o{i}") for i in range(2)])

    def load_small(self, dram_buf, shape, name, dtype=F32):
        P = self.P
        g = P.sbuf(list(shape), dtype, name=name + "_sb")
        P.dma("sp", g[:], dram_buf[:], reads=[dram_buf], writes=[g])
        return g

    def rmsnorm(self, t, g, out=None, ntok=512, xsrc=None):
        P = self.P
        out = out or self.hT
        xs = xsrc or [self.xT[c][t] for c in range(8)]
        ps = self.ps_stat
        for c in range(8):
            sq = self.sq.next()
            P.op("act", lambda e, sq=sq, x=xs[c]: e.activation(out=sq[:, :ntok], in_=x[:, :ntok], func=AF.Square),
                 reads=[xs[c]], writes=[sq])
            P.op("pe", lambda e, sq=sq, c=c: e.matmul(ps[:, :ntok], lhsT=self.ones[:], rhs=sq[:, :ntok], start=(c == 0), stop=(c == 7)),
                 reads=[sq, self.ones], writes=[ps])
        rstd = self.rstd
        P.op("act", lambda e: e.activation(out=rstd[:, :ntok], in_=ps[:, :ntok], func=AF.Sqrt, scale=1.0 / D, bias=self.eps_ap()),
             reads=[ps, self.epsb], writes=[rstd])
        P.op("dve", lambda e: e.reciprocal(out=rstd[:, :ntok], in_=rstd[:, :ntok]), reads=[rstd], writes=[rstd])
        for c in range(8):
            P.op("dve", lambda e, c=c: e.scalar_tensor_tensor(out=out[c][:, :ntok], in0=xs[c][:, :ntok], scalar=g[:, c:c + 1],
                                                               in1=rstd[:, :ntok], op0=ALU.mult, op1=ALU.mult),
                 reads=[xs[c], g, rstd], writes=[out[c]])
        return out

    def eps_ap(self):
        if not hasattr(self, "epsb"):
            self.epsb = self.P.sbuf([128, 1], F32, name="epsb")
            self.P.op("dve", lambda e: e.memset(self.epsb[:], EPS), writes=[self.epsb])
        return self.epsb[:, 0:1]

    def ffn(self, t, g, w_in, w_out, st):
        P = self.P
        hT = self.rmsnorm(t, g)
        win_v = w_in.t.ap().rearrange("(kc p) n -> p kc n", p=128)
        wout_v = w_out.t.ap().rearrange("(j p) n -> p j n", p=128)
        actT = st["actT"]
        for j2 in range(NFF // 2):
            wg = st["wg"].next()
            wu = st["wu"].next()
            P.dma("pool", wg[:], win_v[:, :, j2 * 256:(j2 + 1) * 256], reads=[w_in], writes=[wg])
            P.dma("pool", wu[:], win_v[:, :, DFF + j2 * 256:DFF + (j2 + 1) * 256], reads=[w_in], writes=[wu])
            for jj in range(2):
                j = 2 * j2 + jj
                pg = self.ps_a.next()
                pu = self.ps_b.next()
                for kc in range(8):
                    P.op("pe", lambda e, pg=pg, wg=wg, kc=kc, jj=jj: e.matmul(pg[:], lhsT=wg[:, kc, jj * 128:(jj + 1) * 128], rhs=hT[kc][:],
                                                                             start=(kc == 0), stop=(kc == 7)),
                         reads=[wg, hT[kc]], writes=[pg])
                for kc in range(8):
                    P.op("pe", lambda e, pu=pu, wu=wu, kc=kc, jj=jj: e.matmul(pu[:], lhsT=wu[:, kc, jj * 128:(jj + 1) * 128], rhs=hT[kc][:],
                                                                             start=(kc == 0), stop=(kc == 7)),
                         reads=[wu, hT[kc]], writes=[pu])
                sg = st["sg"].next()
                P.op("act", lambda e, sg=sg, pg=pg: e.activation(out=sg[:], in_=pg[:], func=AF.Silu), reads=[pg], writes=[sg])
                P.op("dve", lambda e, sg=sg, pu=pu, j=j: e.tensor_tensor(out=actT[j][:], in0=sg[:], in1=pu[:], op=ALU.mult),
                     reads=[sg, pu], writes=[actT[j]])
        for dc in range(8):
            wo = st["wo"].next()
            P.dma("pool", wo[:], wout_v[:, :, dc * 128:(dc + 1) * 128], reads=[w_out], writes=[wo])
            po = self.ps_o.next()
            for j in range(NFF):
                P.op("pe", lambda e, po=po, wo=wo, j=j: e.matmul(po[:], lhsT=wo[:, j, :], rhs=actT[j][:],
                                                                 start=(j == 0), stop=(j == NFF - 1)),
                     reads=[wo, actT[j]], writes=[po])
            x = self.xT[dc][t]
            P.op("dve", lambda e, po=po, x=x: e.scalar_tensor_tensor(out=x[:], in0=po[:], scalar=0.5, in1=x[:], op0=ALU.mult, op1=ALU.add),
                 reads=[po, x], writes=[x])

    def proj_add(self, t, w_sb, srcT):
        P = self.P
        for dc in range(8):
            po = self.ps_o.next()
            for kc in range(8):
                P.op("pe", lambda e, po=po, kc=kc, dc=dc: e.matmul(po[:], lhsT=w_sb[:, kc, dc * 128:(dc + 1) * 128], rhs=srcT[kc][:],
                                                                   start=(kc == 0), stop=(kc == 7)),
                     reads=[w_sb, srcT[kc]], writes=[po])
            x = self.xT[dc][t]
            P.op("dve", lambda e, po=po, x=x: e.tensor_tensor(out=x[:], in0=po[:], in1=x[:], op=ALU.add), reads=[po, x], writes=[x])

    def load_w_bf16(self, w_sb, w_dram, ncols=1024, c0=0):
        P = self.P
        v = w_dram.t.ap().rearrange("(kc p) n -> p kc n", p=128)
        for s0 in range(0, ncols, 256):
            P.dma("pool", w_sb[:, :, s0:s0 + 256], v[:, :, c0 + s0:c0 + s0 + 256], reads=[w_dram], writes=[w_sb])

    def ffn_state(self):
        P = self.P
        return {
            "actT": [P.sbuf([128, 512], BF16, name=f"actT{j}") for j in range(NFF)],
            "wg": Ring([P.sbuf([128, 8, 256], BF16, name=f"wg{i}") for i in range(2)]),
            "wu": Ring([P.sbuf([128, 8, 256], BF16, name=f"wu{i}") for i in range(2)]),
            "wo": Ring([P.sbuf([128, NFF, 128], BF16, name=f"wo{i}") for i in range(3)]),
            "sg": Ring([P.sbuf([128, 512], F32, name=f"sg{i}") for i in range(2)]),
        }

    def load_x(self, x_dram):
        P = self.P
        v = x_dram.t.ap().rearrange("(c p) n -> p c n", p=128)
        for c in range(8):
            for t in range(self.ntok // 512):
                P.dma("sp", self.xT[c][t][:], v[:, c, t * 512:(t + 1) * 512], reads=[x_dram], writes=[self.xT[c][t]])

    def store_x(self, x_dram):
        P = self.P
        v = x_dram.t.ap().rearrange("(c p) n -> p c n", p=128)
        for c in range(8):
            for t in range(self.ntok // 512):
                P.dma("sp", v[:, c, t * 512:(t + 1) * 512], self.xT[c][t][:], reads=[self.xT[c][t]], writes=[x_dram])


def phase_A(xT_sh, g1, w_in, w_out, gmix):
    P = Prog()
    x_d = P.dram("x", [D, TPC], F32, kind="ExternalInput")
    g1_d = P.dram("g1", [128, 8], F32, kind="ExternalInput")
    gm_d = P.dram("gm", [128, 8], F32, kind="ExternalInput")
    wi_d = P.dram("wi", [D, 2 * DFF], F32, kind="ExternalInput")
    wo_d = P.dram("wo", [DFF, D], F32, kind="ExternalInput")
    xo_d = P.dram("xo", [D, TPC], F32, kind="ExternalOutput")
    ho_d = P.dram("ho", [D, TPC], BF16, kind="ExternalOutput")
    C = TokCtx(P)
    C.eps_ap()
    g1s = C.load_small(g1_d, [128, 8], "g1s")
    gms = C.load_small(gm_d, [128, 8], "gms")
    C.load_x(x_d)
    st = C.ffn_state()
    hov = ho_d.t.ap().rearrange("(c p) n -> p c n", p=128)
    for t in range(TPC // 512):
        C.ffn(t, g1s, wi_d, wo_d, st)
        hT = C.rmsnorm(t, gms)
        for c in range(8):
            P.dma("sp", hov[:, c, t * 512:(t + 1) * 512], hT[c][:], reads=[hT[c]], writes=[ho_d])
    C.store_x(xo_d)
    maps = [{"x": xT_sh[i], "g1": vec128(g1), "gm": vec128(gmix), "wi": w_in, "wo": w_out} for i in range(NCORE)]
    res = run_prog(P, maps)
    return [r["xo"] for r in res], [r["ho"] for r in res]


def phase_C(xT_sh, mixT_sh, w_mo, g_xa, g_mem, memT, wq, wkv, wo, g_f2, w2_in, w2_out, nxt=None, g_final=None, plain=False):
    P = Prog()
    dI = lambda n, shp, dt=F32: P.dram(n, shp, dt, kind="ExternalInput")
    x_d = dI("x", [D, TPC]); mix_d = dI("mix", [D, TPC], BF16); wmo_d = dI("wmo", [D, D])
    gxa_d = dI("gxa", [128, 8]); gmem_d = dI("gmem", [128, 8]); mem_d = dI("mem", [D, NMEM])
    wq_d = dI("wq", [D, D]); wkv_d = dI("wkv", [D, 2 * D]); wo_d = dI("wo", [D, D])
    gf2_d = dI("gf2", [128, 8]); w2i_d = dI("w2i", [D, 2 * DFF]); w2o_d = dI("w2o", [DFF, D])
    if nxt is not None:
        gn1_d = dI("gn1", [128, 8]); wn1i_d = dI("wn1i", [D, 2 * DFF]); wn1o_d = dI("wn1o", [DFF, D]); gnm_d = dI("gnm", [128, 8])
        xo_d = P.dram("xo", [D, TPC], F32, kind="ExternalOutput")
        ho_d = P.dram("ho", [D, TPC], BF16, kind="ExternalOutput")
    elif plain:
        yo_d = P.dram("yo", [D, TPC], F32, kind="ExternalOutput")
    else:
        gfin_d = dI("gfin", [128, 8])
        yo_d = P.dram("yo", [D, TPC], F32, kind="ExternalOutput")
    C = TokCtx(P)
    C.eps_ap()
    gxa = C.load_small(gxa_d, [128, 8], "gxa"); gmem = C.load_small(gmem_d, [128, 8], "gmem"); gf2 = C.load_small(gf2_d, [128, 8], "gf2")
    if nxt is not None:
        gn1 = C.load_small(gn1_d, [128, 8], "gn1"); gnm = C.load_small(gnm_d, [128, 8], "gnm")
    elif not plain:
        gfin = C.load_small(gfin_d, [128, 8], "gfin")
    C.load_x(x_d)
    st = C.ffn_state()
    actT = st["actT"]
    W1 = P.sbuf([128, 8, 1024], BF16, name="W1")
    W2 = P.sbuf([128, 8, 1024], BF16, name="W2")
    NT = TPC // 512
    C.load_w_bf16(W1, wmo_d)
    mixv = mix_d.t.ap().rearrange("(c p) n -> p c n", p=128)
    for t in range(NT):
        src = actT[8:16]
        for c in range(8):
            P.dma("sp", src[c][:], mixv[:, c, t * 512:(t + 1) * 512], reads=[mix_d], writes=[src[c]])
        C.proj_add(t, W1, src)
    memx = [P.sbuf([128, NMEM], F32, name=f"memx{c}") for c in range(8)]
    mhT = [P.sbuf([128, NMEM], BF16, name=f"mhT{c}") for c in range(8)]
    kT = [P.sbuf([128, NMEM], BF16, name=f"kT{c}") for c in range(8)]
    Vm = [P.sbuf([128, D], BF16, name=f"Vm{c}") for c in range(2)]
    memv = mem_d.t.ap().rearrange("(c p) n -> p c n", p=128)
    for c in range(8):
        P.dma("sp", memx[c][:], memv[:, c, :], reads=[mem_d], writes=[memx[c]])
    C.rmsnorm(0, gmem, out=mhT, ntok=NMEM, xsrc=memx)
    wkv_v = wkv_d.t.ap().rearrange("(kc p) n -> p kc n", p=128)
    for s8 in range(4):
        wsl = st["wg"].next()
        P.dma("pool", wsl[:], wkv_v[:, :, s8 * 256:(s8 + 1) * 256], reads=[wkv_d], writes=[wsl])
        for jj in range(2):
            fc = 2 * s8 + jj
            ps = C.ps_a.next()
            for kc in range(8):
                P.op("pe", lambda e, ps=ps, wsl=wsl, kc=kc, jj=jj: e.matmul(ps[:, :NMEM], lhsT=wsl[:, kc, jj * 128:(jj + 1) * 128], rhs=mhT[kc][:],
                                                                           start=(kc == 0), stop=(kc == 7)),
                     reads=[wsl, mhT[kc]], writes=[ps])
            P.op("act", lambda e, ps=ps, fc=fc: e.copy(out=kT[fc][:], in_=ps[:, :NMEM]), reads=[ps], writes=[kT[fc]])
    for s8 in range(4):
        wsl = st["wu"].next()
        P.dma("pool", wsl[:], wkv_v[:, :, D + s8 * 256:D + (s8 + 1) * 256], reads=[wkv_d], writes=[wsl])
        for mc in range(2):
            ps = C.ps_b.next()
            for kc in range(8):
                P.op("pe", lambda e, ps=ps, wsl=wsl, kc=kc, mc=mc: e.matmul(ps[:, :256], lhsT=mhT[kc][:, mc * 128:(mc + 1) * 128], rhs=wsl[:, kc, :],
                                                                           start=(kc == 0), stop=(kc == 7)),
                     reads=[wsl, mhT[kc]], writes=[ps])
            P.op("act", lambda e, ps=ps, mc=mc, s8=s8: e.copy(out=Vm[mc][:, s8 * 256:(s8 + 1) * 256], in_=ps[:, :256]), reads=[ps], writes=[Vm[mc]])
    C.load_w_bf16(W2, wq_d)
    C.load_w_bf16(W1, wo_d)
    pT = [P.sbuf([128, 512], BF16, name=f"pT{i}") for i in range(2)]
    rden = P.sbuf([128, 512], F32, name="rden")
    qT = actT[0:8]
    oT = actT[8:16]
    XS = (D // 4) ** -0.5
    for t in range(NT):
        hT = C.rmsnorm(t, gxa)
        for fc in range(8):
            ps = C.ps_a.next()
            for kc in range(8):
                P.op("pe", lambda e, ps=ps, kc=kc, fc=fc: e.matmul(ps[:], lhsT=W2[:, kc, fc * 128:(fc + 1) * 128], rhs=hT[kc][:],
                                                                   start=(kc == 0), stop=(kc == 7)),
                     reads=[W2, hT[kc]], writes=[ps])
            P.op("act", lambda e, ps=ps, fc=fc: e.copy(out=qT[fc][:], in_=ps[:]), reads=[ps], writes=[qT[fc]])
        for hh in range(4):
            fcs = (2 * hh, 2 * hh + 1)
            for mc in range(2):
                ps = C.ps_a.next()
                for i, fc in enumerate(fcs):
                    P.op("pe", lambda e, ps=ps, fc=fc, mc=mc, i=i: e.matmul(ps[:], lhsT=kT[fc][:, mc * 128:(mc + 1) * 128], rhs=qT[fc][:],
                                                                           start=(i == 0), stop=(i == 1)),
                         reads=[kT[fc], qT[fc]], writes=[ps])
                P.op("act", lambda e, ps=ps, mc=mc: e.activation(out=pT[mc][:], in_=ps[:], func=AF.Exp, scale=XS), reads=[ps], writes=[pT[mc]])
            pd = C.ps_stat
            for mc in range(2):
                P.op("pe", lambda e, mc=mc: e.matmul(pd[:], lhsT=C.ones[:], rhs=pT[mc][:], start=(mc == 0), stop=(mc == 1)),
                     reads=[C.ones, pT[mc]], writes=[pd])
            P.op("dve", lambda e: e.reciprocal(out=rden[:], in_=pd[:]), reads=[pd], writes=[rden])
            for fc in fcs:
                ps = C.ps_b.next()
                for mc in range(2):
                    P.op("pe", lambda e, ps=ps, fc=fc, mc=mc: e.matmul(ps[:], lhsT=Vm[mc][:, fc * 128:(fc + 1) * 128], rhs=pT[mc][:],
                                                                       start=(mc == 0), stop=(mc == 1)),
                         reads=[Vm[mc], pT[mc]], writes=[ps])
                P.op("dve", lambda e, ps=ps, fc=fc: e.tensor_tensor(out=oT[fc][:], in0=ps[:], in1=rden[:], op=ALU.mult),
                     reads=[ps, rden], writes=[oT[fc]])
        C.proj_add(t, W1, oT)
    for t in range(NT):
        C.ffn(t, gf2, w2i_d, w2o_d, st)
        if nxt is not None:
            C.ffn(t, gn1, wn1i_d, wn1o_d, st)
            hT = C.rmsnorm(t, gnm)
            hov = ho_d.t.ap().rearrange("(c p) n -> p c n", p=128)
            for c in range(8):
                P.dma("sp", hov[:, c, t * 512:(t + 1) * 512], hT[c][:], reads=[hT[c]], writes=[ho_d])
        elif not plain:
            ps = C.ps_stat
            for c in range(8):
                sq = C.sq.next()
                x = C.xT[c][t]
                P.op("act", lambda e, sq=sq, x=x: e.activation(out=sq[:], in_=x[:], func=AF.Square), reads=[x], writes=[sq])
                P.op("pe", lambda e, sq=sq, c=c: e.matmul(ps[:], lhsT=C.ones[:], rhs=sq[:], start=(c == 0), stop=(c == 7)),
                     reads=[sq, C.ones], writes=[ps])
            P.op("act", lambda e: e.activation(out=C.rstd[:], in_=ps[:], func=AF.Sqrt, scale=1.0 / D, bias=C.eps_ap()),
                 reads=[ps, C.epsb], writes=[C.rstd])
            P.op("dve", lambda e: e.reciprocal(out=C.rstd[:], in_=C.rstd[:]), reads=[C.rstd], writes=[C.rstd])
            for c in range(8):
                x = C.xT[c][t]
                P.op("dve", lambda e, c=c, x=x: e.scalar_tensor_tensor(out=x[:], in0=x[:], scalar=gfin[:, c:c + 1], in1=C.rstd[:],
                                                                        op0=ALU.mult, op1=ALU.mult),
                     reads=[x, gfin, C.rstd], writes=[x])
    if nxt is not None:
        C.store_x(xo_d)
    else:
        C.store_x(yo_d)
    maps = []
    for i in range(NCORE):
        m = {"x": xT_sh[i], "mix": mixT_sh[i], "wmo": w_mo, "gxa": vec128(g_xa), "gmem": vec128(g_mem), "mem": memT,
             "wq": wq, "wkv": wkv, "wo": wo, "gf2": vec128(g_f2), "w2i": w2_in, "w2o": w2_out}
        if nxt is not None:
            m.update({"gn1": vec128(nxt[0]), "wn1i": nxt[1], "wn1o": nxt[2], "gnm": vec128(nxt[3])})
        elif not plain:
            m["gfin"] = vec128(g_final)
        maps.append(m)
    res = run_prog(P, maps)
    if nxt is not None:
        return [r["xo"] for r in res], [r["ho"] for r in res]
    return [r["yo"] for r in res]


def head_proj(h_allT, w_cores, M):
    Mp = 640
    if Mp != M:
        w_cores = [np.ascontiguousarray(np.concatenate([w, np.zeros((D, Mp - M), np.float32)], axis=1)) for w in w_cores]
        M = Mp
    P = Prog()
    h_d = P.dram("h", [D, S], BF16, kind="ExternalInput")
    w_d = P.dram("w", [D, M], F32, kind="ExternalInput")
    o_d = P.dram("o", [M, S], F32, kind="ExternalOutput")
    W = P.sbuf([128, 8, M], BF16, name="Wp")
    wv = w_d.t.ap().rearrange("(kc p) n -> p kc n", p=128)
    P.dma("pool", W[:], wv, reads=[w_d], writes=[W])
    hv = h_d.t.ap().rearrange("(kc p) n -> p kc n", p=128)
    hb = Ring([P.sbuf([128, 8, 512], BF16, name=f"hb{i}") for i in range(3)])
    pss = Ring([P.psum([128, 512], F32, name=f"pp{i}") for i in range(4)])
    stg = Ring([P.sbuf([128, 512], F32, name=f"stg{i}") for i in range(4)])
    groups = [(m0, min(m0 + 128, M)) for m0 in range(0, M, 128)]
    k = 0
    for t in range(S // 512):
        h = hb.next()
        P.dma("sp", h[:], hv[:, :, t * 512:(t + 1) * 512], reads=[h_d], writes=[h])
        for (m0, m1) in groups:
            ps = pss.next()
            m = m1 - m0
            for kc in range(8):
                P.op("pe", lambda e, ps=ps, h=h, kc=kc, m0=m0, m1=m1, m=m: e.matmul(ps[0:m, :], lhsT=W[:, kc, m0:m1], rhs=h[:, kc, :],
                                                                                     start=(kc == 0), stop=(kc == 7)),
                     reads=[W, h], writes=[ps])
            sg = stg.next()
            eng = "act" if k % 2 == 0 else "dve"
            k += 1
            if eng == "act":
                P.op("act", lambda e, ps=ps, sg=sg, m=m: e.copy(out=sg[0:m, :], in_=ps[0:m, :]), reads=[ps], writes=[sg])
            else:
                P.op("dve", lambda e, ps=ps, sg=sg, m=m: e.tensor_copy(out=sg[0:m, :], in_=ps[0:m, :]), reads=[ps], writes=[sg])
            P.dma("sp", o_d[m0:m1, t * 512:(t + 1) * 512], sg[0:m, :], reads=[sg], writes=[o_d])
    res = run_prog(P, [{"h": h_allT, "w": w_cores[i]} for i in range(NCORE)])
    return [r["o"] for r in res]


def banded_attn(q3, k3, v3, maskT):
    P = Prog()
    q_d = P.dram("q", [3, 64, S], F32, kind="ExternalInput")
    k_d = P.dram("k", [3, 64, S], F32, kind="ExternalInput")
    v_d = P.dram("v", [3, S, 64], F32, kind="ExternalInput")
    m_d = P.dram("m", [128, 256], F32, kind="ExternalInput")
    o_d = P.dram("o", [3, 65, S], F32, kind="ExternalOutput")
    qT = P.sbuf([64, S], BF16, name="qT")
    kT = P.sbuf([64, S], BF16, name="kT")
    Vx = P.sbuf([128, 128, 65], BF16, name="Vx")
    mk = P.sbuf([128, 256], F32, name="mk")
    P.dma("sp", mk[:], m_d[:], reads=[m_d], writes=[mk])
    Oacc = [P.sbuf([65, 2048], F32, name=f"Oacc{i}") for i in range(8)]
    sps = Ring([P.psum([128, 256], F32, name=f"sps{i}") for i in range(2)])
    ops_ = Ring([P.psum([128, 128], F32, name=f"ops{i}") for i in range(2)])
    ssb = Ring([P.sbuf([128, 256], F32, name=f"ssb{i}") for i in range(2)])
    pTr = Ring([P.sbuf([128, 256], BF16, name=f"pTr{i}") for i in range(2)])
    for pi, dil in enumerate((1, 4, 16)):
        nseg = 128 // dil
        for a in range(0, S, 2048):
            P.dma("pool", qT[:, a:a + 2048], q_d[pi, :, a:a + 2048], reads=[q_d], writes=[qT])
            P.dma("pool", kT[:, a:a + 2048], k_d[pi, :, a:a + 2048], reads=[k_d], writes=[kT])
        vv = v_d.t.ap()[pi].rearrange("(b p) f -> p b f", p=128)
        for b0 in range(0, 128, 32):
            P.dma("pool", Vx[:, b0:b0 + 32, 0:64], vv[:, b0:b0 + 32, :], reads=[v_d], writes=[Vx])
        P.op("dve", lambda e: e.memset(Vx[:, :, 64:65], 1.0), writes=[Vx])
        for b in range(128):
            first = (b % nseg == 0)
            c0 = 128 if first else 0
            sp_ = sps.next()
            qs = slice(b * 128, (b + 1) * 128)
            if not first:
                P.op("pe", lambda e, sp_=sp_, b=b, qs=qs: e.matmul(sp_[:, 0:128], lhsT=kT[:, (b - 1) * 128:b * 128], rhs=qT[:, qs], start=True, stop=True),
                     reads=[kT, qT], writes=[sp_])
            P.op("pe", lambda e, sp_=sp_, qs=qs: e.matmul(sp_[:, 128:256], lhsT=kT[:, qs], rhs=qT[:, qs], start=True, stop=True),
                 reads=[kT, qT], writes=[sp_])
            sb = ssb.next()
            P.op("dve", lambda e, sb=sb, sp_=sp_, c0=c0: e.tensor_tensor(out=sb[:, c0:256], in0=sp_[:, c0:256], in1=mk[:, c0:256], op=ALU.add),
                 reads=[sp_, mk], writes=[sb])
            pT = pTr.next()
            P.op("act", lambda e, sb=sb, pT=pT, c0=c0: e.activation(out=pT[:, c0:256], in_=sb[:, c0:256], func=AF.Exp, scale=0.125),
                 reads=[sb], writes=[pT])
            op_ = ops_.next()
            if not first:
                P.op("pe", lambda e, op_=op_, pT=pT, b=b: e.matmul(op_[0:65, :], lhsT=Vx[:, b - 1, :], rhs=pT[:, 0:128], start=True, stop=False),
                     reads=[Vx, pT], writes=[op_])
            P.op("pe", lambda e, op_=op_, pT=pT, b=b, first=first: e.matmul(op_[0:65, :], lhsT=Vx[:, b, :], rhs=pT[:, 128:256], start=first, stop=True),
                 reads=[Vx, pT], writes=[op_])
            oa = Oacc[b // 16]
            P.op("act", lambda e, oa=oa, op_=op_, b=b: e.copy(out=oa[:, (b % 16) * 128:(b % 16 + 1) * 128], in_=op_[0:65, :]),
                 reads=[op_], writes=[oa])
            if b % 16 == 15:
                P.dma("sp", o_d[pi, :, (b // 16) * 2048:(b // 16 + 1) * 2048], oa[:], reads=[oa], writes=[o_d])
    res = run_prog(P, [{"q": q3[i], "k": k3[i], "v": v3[i], "m": maskT} for i in range(NCORE)])
    return [r["o"] for r in res]


def lru_combine(xr_pad, gr, cw, cb, wa, ba, wx, bx, lam, num3, den3):
    P = Prog()
    dI = lambda n, shp, dt=F32: P.dram(n, shp, dt, kind="ExternalInput")
    xr_d = dI("xr", [64, S + 3]); gr_d = dI("gr", [64, S]); cw_d = dI("cw", [64, 4]); cb_d = dI("cb", [64, 1])
    wa_d = dI("wa", [64, 64]); ba_d = dI("ba", [64, 1]); wx_d = dI("wx", [64, 64]); bx_d = dI("bx", [64, 1]); lam_d = dI("lam", [64, 1])
    n_d = dI("num", [3, 64, S]); d_d = dI("den", [3, 64, S])
    at_d = P.dram("attn", [64, S], BF16, kind="ExternalOutput")
    y_d = P.dram("y", [64, S], BF16, kind="ExternalOutput")

    def small(dr, shp, nm, q="sp", dt=F32):
        b = P.sbuf(shp, dt, name=nm)
        P.dma(q, b[:], dr[:], reads=[dr], writes=[b])
        return b
    cws = small(cw_d, [64, 4], "cws"); cbs = small(cb_d, [64, 1], "cbs"); bas = small(ba_d, [64, 1], "bas"); bxs = small(bx_d, [64, 1], "bxs")
    lams = small(lam_d, [64, 1], "lams")
    was = small(wa_d, [64, 64], "was", q="pool", dt=BF16); wxs = small(wx_d, [64, 64], "wxs", q="pool", dt=BF16)
    one1 = P.sbuf([64, 1], F32, name="one1")
    P.op("dve", lambda e: e.memset(one1[:], 1.0), writes=[one1])
    c8 = P.sbuf([64, 1], F32, name="c8")
    P.op("act", lambda e: e.activation(out=c8[:], in_=lams[:], func=AF.Exp, scale=-1.0), reads=[lams], writes=[c8])
    P.op("act", lambda e: e.activation(out=c8[:], in_=c8[:], func=AF.Ln, bias=one1[:, 0:1], scale=1.0), reads=[c8, one1], writes=[c8])
    P.op("dve", lambda e: e.tensor_scalar(out=c8[:], in0=c8[:], scalar1=-8.0, scalar2=None, op0=ALU.mult), reads=[c8], writes=[c8])
    CH = 2048
    mk = lambda nm, n=2, w=CH, dt=F32: Ring([P.sbuf([64, w], dt, name=f"{nm}{i}") for i in range(n)])
    xr_r = mk("xr", 2, CH + 3); gr_r = mk("gr"); xc_r = mk("xc", 1); xcb_r = mk("xcb", 1, CH, BF16)
    r_r = mk("r", 1); i_r = mk("i", 1); a_r = mk("a", 1); t_r = mk("t", 1); hs_r = mk("hs"); u_r = mk("u", 1)
    yb_r = mk("yb", 2, CH, BF16); ab_r = mk("ab", 2, CH, BF16)
    n_r = [mk(f"n{j}", 1) for j in range(3)]; dn_r = [mk(f"dn{j}", 1) for j in range(3)]
    psr = Ring([P.psum([64, 512], F32, name=f"psr{i}") for i in range(2)])
    psi = Ring([P.psum([64, 512], F32, name=f"psi{i}") for i in range(2)])
    hs_prev = None
    for ch in range(S // CH):
        c0 = ch * CH
        xr = xr_r.next(); gr_ = gr_r.next(); xc = xc_r.next(); xcb = xcb_r.next()
        P.dma("sp", xr[:], xr_d[:, c0:c0 + CH + 3], reads=[xr_d], writes=[xr])
        P.dma("sp", gr_[:], gr_d[:, c0:c0 + CH], reads=[gr_d], writes=[gr_])
        P.op("dve", lambda e, xc=xc, xr=xr: e.tensor_scalar(out=xc[:], in0=xr[:, 3:CH + 3], scalar1=cws[:, 3:4], scalar2=cbs[:, 0:1], op0=ALU.mult, op1=ALU.add),
             reads=[xr, cws, cbs], writes=[xc])
        for k in range(3):
            P.op("dve", lambda e, xc=xc, xr=xr, k=k: e.scalar_tensor_tensor(out=xc[:], in0=xr[:, k:k + CH], scalar=cws[:, k:k + 1], in1=xc[:], op0=ALU.mult, op1=ALU.add),
                 reads=[xr, cws, xc], writes=[xc])
        P.op("act", lambda e, xc=xc, xcb=xcb: e.copy(out=xcb[:], in_=xc[:]), reads=[xc], writes=[xcb])
        r = r_r.next(); ii = i_r.next()
        for sub in range(CH // 512):
            ss = slice(sub * 512, (sub + 1) * 512)
            pr = psr.next(); pi_ = psi.next()
            P.op("pe", lambda e, pr=pr, xcb=xcb, ss=ss: e.matmul(pr[:], lhsT=was[:], rhs=xcb[:, ss], start=True, stop=True), reads=[was, xcb], writes=[pr])
            P.op("pe", lambda e, pi_=pi_, xcb=xcb, ss=ss: e.matmul(pi_[:], lhsT=wxs[:], rhs=xcb[:, ss], start=True, stop=True), reads=[wxs, xcb], writes=[pi_])
            P.op("act", lambda e, pr=pr, r=r, ss=ss: e.activation(out=r[:, ss], in_=pr[:], func=AF.Sigmoid, bias=bas[:, 0:1], scale=1.0), reads=[pr, bas], writes=[r])
            P.op("act", lambda e, pi_=pi_, ii=ii, ss=ss: e.activation(out=ii[:, ss], in_=pi_[:], func=AF.Sigmoid, bias=bxs[:, 0:1], scale=1.0), reads=[pi_, bxs], writes=[ii])
        a = a_r.next(); tt = t_r.next()
        P.op("act", lambda e, a=a, r=r: e.activation(out=a[:], in_=r[:], func=AF.Exp, scale=c8[:, 0:1]), reads=[r, c8], writes=[a])
        P.op("pool", lambda e, a=a, tt=tt: e.tensor_tensor(out=tt[:], in0=a[:], in1=a[:], op=ALU.mult), reads=[a], writes=[tt])
        P.op("act", lambda e, tt=tt: e.activation(out=tt[:], in_=tt[:], func=AF.Sqrt, bias=one1[:, 0:1], scale=-1.0), reads=[tt, one1], writes=[tt])
        P.op("pool", lambda e, tt=tt, ii=ii: e.tensor_tensor(out=tt[:], in0=tt[:], in1=ii[:], op=ALU.mult), reads=[tt, ii], writes=[tt])
        P.op("dve", lambda e, tt=tt, xc=xc: e.tensor_tensor(out=tt[:], in0=tt[:], in1=xc[:], op=ALU.mult), reads=[tt, xc], writes=[tt])
        hs = hs_r.next()
        if hs_prev is None:
            P.op("dve", lambda e, hs=hs, a=a, tt=tt: e.tensor_tensor_scan(out=hs[:], data0=a[:], data1=tt[:], initial=0.0, op0=ALU.mult, op1=ALU.add),
                 reads=[a, tt], writes=[hs])
        else:
            P.op("dve", lambda e, hs=hs, a=a, tt=tt, hp=hs_prev: e.tensor_tensor_scan(out=hs[:], data0=a[:], data1=tt[:], initial=hp[:, CH - 1:CH], op0=ALU.mult, op1=ALU.add),
                 reads=[a, tt, hs_prev], writes=[hs])
        hs_prev = hs
        u = u_r.next()
        P.op("pool", lambda e, u=u, g=gr_: e.tensor_tensor(out=u[:], in0=g[:], in1=g[:], op=ALU.mult), reads=[gr_], writes=[u])
        P.op("dve", lambda e, u=u: e.tensor_scalar(out=u[:], in0=u[:], scalar1=0.044715, scalar2=1.0, op0=ALU.mult, op1=ALU.add), reads=[u], writes=[u])
        P.op("pool", lambda e, u=u, g=gr_: e.tensor_tensor(out=u[:], in0=u[:], in1=g[:], op=ALU.mult), reads=[u, gr_], writes=[u])
        P.op("act", lambda e, u=u: e.activation(out=u[:], in_=u[:], func=AF.Sigmoid, scale=1.5957691216057308), reads=[u], writes=[u])
        P.op("pool", lambda e, u=u, g=gr_: e.tensor_tensor(out=u[:], in0=u[:], in1=g[:], op=ALU.mult), reads=[u, gr_], writes=[u])
        yb = yb_r.next()
        P.op("dve", lambda e, u=u, hs=hs, yb=yb: e.tensor_tensor(out=yb[:], in0=u[:], in1=hs[:], op=ALU.mult), reads=[u, hs], writes=[yb])
        P.dma("sp", y_d[:, c0:c0 + CH], yb[:], reads=[yb], writes=[y_d])
        ns = [n_r[j].next() for j in range(3)]; ds = [dn_r[j].next() for j in range(3)]
        for j in range(3):
            P.dma("sp", ns[j][:], n_d[j, :, c0:c0 + CH], reads=[n_d], writes=[ns[j]])
            P.dma("sp", ds[j][:], d_d[j, :, c0:c0 + CH], reads=[d_d], writes=[ds[j]])
        P.op("pool", lambda e, ns=ns: e.tensor_tensor(out=ns[0][:], in0=ns[0][:], in1=ns[1][:], op=ALU.add), reads=[ns[0], ns[1]], writes=[ns[0]])
        P.op("pool", lambda e, ns=ns: e.tensor_tensor(out=ns[0][:], in0=ns[0][:], in1=ns[2][:], op=ALU.add), reads=[ns[0], ns[2]], writes=[ns[0]])
        P.op("pool", lambda e, ds=ds: e.tensor_tensor(out=ds[0][:], in0=ds[0][:], in1=ds[1][:], op=ALU.add), reads=[ds[0], ds[1]], writes=[ds[0]])
        P.op("dve", lambda e, ds=ds: e.tensor_tensor(out=ds[0][:], in0=ds[0][:], in1=ds[2][:], op=ALU.add), reads=[ds[0], ds[2]], writes=[ds[0]])
        P.op("dve", lambda e, ds=ds: e.reciprocal(out=ds[0][:], in_=ds[0][:]), reads=[ds[0]], writes=[ds[0]])
        ab = ab_r.next()
        P.op("dve", lambda e, ab=ab, ns=ns, ds=ds: e.tensor_tensor(out=ab[:], in0=ns[0][:], in1=ds[0][:], op=ALU.mult), reads=[ns[0], ds[0]], writes=[ab])
        P.dma("sp", at_d[:, c0:c0 + CH], ab[:], reads=[ab], writes=[at_d])
    maps = [{"xr": xr_pad[i], "gr": gr[i], "cw": cw[i], "cb": cb[i], "wa": wa[i], "ba": ba[i], "wx": wx[i], "bx": bx[i], "lam": lam[i],
             "num": num3[i], "den": den3[i]} for i in range(NCORE)]
    res = run_prog(P, maps)
    return [r["attn"] for r in res], [r["y"] for r in res]


def layer0_mixer(h_allT, inp):
    c = np.ascontiguousarray
    w_in = inp["ab_w_in"][0]
    wc = [c(np.concatenate([w_in[:, o + 64 * i:o + 64 * (i + 1)] for o in (0, 512, 1024, 1536, 2048)], axis=1)) for i in range(NCORE)]
    proj = head_proj(h_allT, wc, 320)
    perms = [np.arange(S).reshape(S // d, d).T.reshape(-1) for d in (1, 4, 16)]
    q3 = [c(np.stack([p[0:64][:, pm] for pm in perms])) for p in proj]
    k3 = [c(np.stack([p[64:128][:, pm] for pm in perms])) for p in proj]
    v3 = [c(np.stack([p[128:192][:, pm].T for pm in perms])) for p in proj]
    pp = np.arange(128)[:, None]; qq = np.arange(128)[None, :]
    maskT = np.concatenate([np.where(pp >= qq, 0.0, NEG), np.where(pp <= qq, 0.0, NEG)], axis=1).astype(np.float32)
    O = banded_attn(q3, k3, v3, c(maskT))
    inv = [np.argsort(pm) for pm in perms]
    num3 = [c(np.stack([o[j, 0:64][:, inv[j]] for j in range(3)])) for o in O]
    den3 = [c(np.stack([np.broadcast_to(o[j, 64:65][:, inv[j]], (64, S)) for j in range(3)])) for o in O]
    xr_pad = [c(np.concatenate([np.zeros((64, 3), np.float32), p[192:256]], axis=1)) for p in proj]
    gr = [c(p[256:320]) for p in proj]
    sl = lambda v, i: c(np.asarray(v)[64 * i:64 * (i + 1)].reshape(64, 1))
    cw = [c(inp["lru_conv_w"][0][:, 64 * i:64 * (i + 1)].T) for i in range(NCORE)]
    cb = [sl(inp["lru_conv_b"][0], i) for i in range(NCORE)]
    wa = [c(inp["lru_w_a"][0][i]) for i in range(NCORE)]
    wx = [c(inp["lru_w_x"][0][i]) for i in range(NCORE)]
    ba = [sl(inp["lru_b_a"][0], i) for i in range(NCORE)]
    bx = [sl(inp["lru_b_x"][0], i) for i in range(NCORE)]
    lam = [sl(inp["lru_lambda"][0], i) for i in range(NCORE)]
    attn, y = lru_combine(xr_pad, gr, cw, cb, wa, ba, wx, bx, lam, num3, den3)
    mixT = np.concatenate(attn + y, axis=0)
    return mixT


def gdn_prep(qkv_pad, cw3, a128, b128, scal):
    P = Prog()
    dI = lambda n, shp, dt=F32: P.dram(n, shp, dt, kind="ExternalInput")
    x_d = dI("x", [3, 128, S + 3]); cw_d = dI("cw", [128, 12]); a_d = dI("a", [128, 128]); b_d = dI("b", [128, 128]); sc_d = dI("sc", [128, 2])
    o_d = P.dram("o", [3, 128, S], F32, kind="ExternalOutput")
    gc_d = P.dram("gc", [128, 128], F32, kind="ExternalOutput")
    be_d = P.dram("be", [128, 128], F32, kind="ExternalOutput")
    cws = P.sbuf([128, 12], F32, name="cws"); P.dma("sp", cws[:], cw_d[:], reads=[cw_d], writes=[cws])
    sc = P.sbuf([128, 2], F32, name="scs"); P.dma("sp", sc[:], sc_d[:], reads=[sc_d], writes=[sc])
    one1 = P.sbuf([128, 1], F32, name="one1"); P.op("dve", lambda e: e.memset(one1[:], 1.0), writes=[one1])
    eps1 = P.sbuf([128, 1], F32, name="eps1"); P.op("dve", lambda e: e.memset(eps1[:], 1e-6), writes=[eps1])
    ones = P.sbuf([128, 128], BF16, name="ones"); P.op("dve", lambda e: e.memset(ones[:], 1.0), writes=[ones])
    av = P.sbuf([128, 128], F32, name="av"); bv = P.sbuf([128, 128], F32, name="bv"); rm = P.sbuf([128, 128], F32, name="rm")
    gcv = P.sbuf([128, 128], F32, name="gcv"); nea = P.sbuf([128, 1], F32, name="nea")
    P.dma("sp", av[:], a_d[:], reads=[a_d], writes=[av]); P.dma("sp", bv[:], b_d[:], reads=[b_d], writes=[bv])
    P.op("act", lambda e: e.activation(out=nea[:], in_=sc[:, 0:1], func=AF.Exp), reads=[sc], writes=[nea])
    P.op("dve", lambda e: e.tensor_scalar(out=nea[:], in0=nea[:], scalar1=-1.0, scalar2=None, op0=ALU.mult), reads=[nea], writes=[nea])
    P.op("act", lambda e: e.activation(out=av[:], in_=av[:], func=AF.Exp, bias=sc[:, 1:2], scale=1.0), reads=[av, sc], writes=[av])
    P.op("act", lambda e: e.activation(out=av[:], in_=av[:], func=AF.Ln, bias=one1[:, 0:1], scale=1.0), reads=[av, one1], writes=[av])
    P.op("dve", lambda e: e.tensor_scalar(out=av[:], in0=av[:], scalar1=nea[:, 0:1], scalar2=None, op0=ALU.mult), reads=[av, nea], writes=[av])
    P.op("act", lambda e: e.activation(out=bv[:], in_=bv[:], func=AF.Sigmoid), reads=[bv], writes=[bv])
    P.op("dve", lambda e: e.memset(rm[:], 1.0), writes=[rm])
    P.op("dve", lambda e: e.memset(rm[:, 0:128:64], 0.0), writes=[rm])
    P.op("dve", lambda e: e.tensor_tensor_scan(out=gcv[:], data0=rm[:], data1=av[:], initial=0.0, op0=ALU.mult, op1=ALU.add), reads=[rm, av], writes=[gcv])
    P.dma("sp", gc_d[:], gcv[:], reads=[gcv], writes=[gc_d]); P.dma("sp", be_d[:], bv[:], reads=[bv], writes=[be_d])
    xin = Ring([P.sbuf([128, 515], F32, name=f"xin{i}") for i in range(3)])
    acc = Ring([P.sbuf([128, 512], F32, name=f"acc{i}") for i in range(3)])
    sqr = Ring([P.sbuf([128, 512], BF16, name=f"sqr{i}") for i in range(2)])
    rnr = Ring([P.sbuf([128, 512], F32, name=f"rnr{i}") for i in range(2)])
    pss = Ring([P.psum([128, 512], F32, name=f"pss{i}") for i in range(2)])
    for t in range(S // 512):
        for w in range(3):
            xi = xin.next(); ac = acc.next()
            P.dma("sp", xi[:], x_d[w, :, t * 512:t * 512 + 515], reads=[x_d], writes=[xi])
            P.op("dve", lambda e, ac=ac, xi=xi, w=w: e.tensor_scalar(out=ac[:], in0=xi[:, 3:515], scalar1=cws[:, 4 * w + 3:4 * w + 4], scalar2=None, op0=ALU.mult),
                 reads=[xi, cws], writes=[ac])
            for k in range(3):
                eng = "dve" if k != 1 else "pool"
                if eng == "dve":
                    P.op("dve", lambda e, ac=ac, xi=xi, w=w, k=k: e.scalar_tensor_tensor(out=ac[:], in0=xi[:, k:k + 512], scalar=cws[:, 4 * w + k:4 * w + k + 1], in1=ac[:], op0=ALU.mult, op1=ALU.add),
                         reads=[xi, cws, ac], writes=[ac])
                else:
                    P.op("dve", lambda e, ac=ac, xi=xi, w=w, k=k: e.scalar_tensor_tensor(out=ac[:], in0=xi[:, k:k + 512], scalar=cws[:, 4 * w + k:4 * w + k + 1], in1=ac[:], op0=ALU.mult, op1=ALU.add),
                         reads=[xi, cws, ac], writes=[ac])
            P.op("act", lambda e, ac=ac: e.activation(out=ac[:], in_=ac[:], func=AF.Silu), reads=[ac], writes=[ac])
            if w < 2:
                sq = sqr.next(); ps = pss.next(); rn = rnr.next()
                P.op("act", lambda e, ac=ac, sq=sq: e.activation(out=sq[:], in_=ac[:], func=AF.Square), reads=[ac], writes=[sq])
                P.op("pe", lambda e, ps=ps, sq=sq: e.matmul(ps[:], lhsT=ones[:], rhs=sq[:], start=True, stop=True), reads=[ones, sq], writes=[ps])
                P.op("act", lambda e, ps=ps, rn=rn: e.activation(out=rn[:], in_=ps[:], func=AF.Sqrt, bias=eps1[:, 0:1], scale=1.0), reads=[ps, eps1], writes=[rn])
                P.op("dve", lambda e, rn=rn: e.reciprocal(out=rn[:], in_=rn[:]), reads=[rn], writes=[rn])
                sc_ = (128.0 ** -0.5) if w == 0 else 1.0
                P.op("dve", lambda e, ac=ac, rn=rn, sc_=sc_: e.scalar_tensor_tensor(out=ac[:], in0=ac[:], scalar=sc_, in1=rn[:], op0=ALU.mult, op1=ALU.mult),
                     reads=[ac, rn], writes=[ac])
            P.dma("sp", o_d[w, :, t * 512:(t + 1) * 512], ac[:], reads=[ac], writes=[o_d])
    maps = [{"x": qkv_pad[i], "cw": cw3[i], "a": a128[i], "b": b128[i], "sc": scal[i]} for i in range(NCORE)]
    res = run_prog(P, maps)
    return [r["o"] for r in res], [r["gc"] for r in res], [r["be"] for r in res]


def gdn_amat(qk2, gcr, ber, gcc, mstrict, mincl):
    P = Prog()
    dI = lambda n, shp, dt=F32: P.dram(n, shp, dt, kind="ExternalInput")
    x_d = dI("x", [2, 128, S]); gcr_d = dI("gcr", [64, S]); ber_d = dI("ber", [64, S]); gcc_d = dI("gcc", [64, 256])
    ms_d = dI("ms", [64, 64]); mi_d = dI("mi", [64, 64])
    at_d = P.dram("at", [64, 256, 64], F32, kind="ExternalOutput")
    qk_d = P.dram("qk", [64, 256, 64], F32, kind="ExternalOutput")
    gcc_s = P.sbuf([64, 256], F32, name="gccs"); P.dma("sp", gcc_s[:], gcc_d[:], reads=[gcc_d], writes=[gcc_s])
    ms = P.sbuf([64, 64], F32, name="mss"); P.dma("sp", ms[:], ms_d[:], reads=[ms_d], writes=[ms])
    mi = P.sbuf([64, 64], F32, name="mis"); P.dma("sp", mi[:], mi_d[:], reads=[mi_d], writes=[mi])
    R = lambda nm, shp, dt=F32, n=2: Ring([P.sbuf(shp, dt, name=f"{nm}{i}") for i in range(n)])
    qb_r = R("qb", [128, 512], BF16); kb_r = R("kb", [128, 512], BF16); gr_r = R("gr", [64, 8, 64]); br_r = R("br", [64, 8, 64])
    dd_r = R("dd", [64, 8, 64]); es_r = R("es", [64, 8, 64]); ei_r = R("ei", [64, 8, 64]); at_r = R("ato", [64, 8, 64]); qo_r = R("qo", [64, 8, 64])
    pk_r = Ring([P.psum([64, 8, 64], F32, name=f"pk{i}") for i in range(2)])
    pq_r = Ring([P.psum([64, 8, 64], F32, name=f"pq{i}") for i in range(2)])
    for t in range(S // 512):
        n0 = t * 8
        ts_ = slice(t * 512, (t + 1) * 512)
        qb = qb_r.next(); kb = kb_r.next(); gr_ = gr_r.next(); br = br_r.next()
        P.dma("pool", qb[:], x_d[0, :, ts_], reads=[x_d], writes=[qb])
        P.dma("pool", kb[:], x_d[1, :, ts_], reads=[x_d], writes=[kb])
        P.dma("sp", gr_[:], gcr_d[:, ts_].rearrange("p (n i) -> p n i", i=64), reads=[gcr_d], writes=[gr_])
        P.dma("sp", br[:], ber_d[:, ts_].rearrange("p (n i) -> p n i", i=64), reads=[ber_d], writes=[br])
        pk = pk_r.next(); pq = pq_r.next()
        for n in range(8):
            cs = slice(n * 64, (n + 1) * 64)
            P.op("pe", lambda e, pk=pk, kb=kb, n=n, cs=cs: e.matmul(pk[:, n, :], lhsT=kb[:, cs], rhs=kb[:, cs], start=True, stop=True), reads=[kb], writes=[pk])
            P.op("pe", lambda e, pq=pq, kb=kb, qb=qb, n=n, cs=cs: e.matmul(pq[:, n, :], lhsT=kb[:, cs], rhs=qb[:, cs], start=True, stop=True), reads=[kb, qb], writes=[pq])
        dd = dd_r.next(); es = es_r.next(); ei = ei_r.next(); ato = at_r.next(); qo = qo_r.next()
        P.op("dve", lambda e, dd=dd, gr_=gr_, n0=n0: e.tensor_tensor(out=dd[:], in0=gr_[:], in1=gcc_s[:, n0:n0 + 8].unsqueeze(2).broadcast_to([64, 8, 64]), op=ALU.subtract),
             reads=[gr_, gcc_s], writes=[dd])
        P.op("dve", lambda e, dd=dd, es=es: e.tensor_tensor(out=es[:], in0=dd[:], in1=ms[:].unsqueeze(1).broadcast_to([64, 8, 64]), op=ALU.add), reads=[dd, ms], writes=[es])
        P.op("pool", lambda e, dd=dd, ei=ei: e.tensor_tensor(out=ei[:], in0=dd[:], in1=mi[:].unsqueeze(1).broadcast_to([64, 8, 64]), op=ALU.add), reads=[dd, mi], writes=[ei])
        P.op("act", lambda e, es=es: e.activation(out=es[:], in_=es[:], func=AF.Exp), reads=[es], writes=[es])
        P.op("act", lambda e, ei=ei: e.activation(out=ei[:], in_=ei[:], func=AF.Exp), reads=[ei], writes=[ei])
        P.op("pool", lambda e, es=es, br=br: e.tensor_tensor(out=es[:], in0=es[:], in1=br[:], op=ALU.mult), reads=[es, br], writes=[es])
        P.op("dve", lambda e, ato=ato, pk=pk, es=es: e.tensor_tensor(out=ato[:], in0=pk[:], in1=es[:], op=ALU.mult), reads=[pk, es], writes=[ato])
        P.op("dve", lambda e, qo=qo, pq=pq, ei=ei: e.tensor_tensor(out=qo[:], in0=pq[:], in1=ei[:], op=ALU.mult), reads=[pq, ei], writes=[qo])
        P.dma("sp", at_d[:, n0:n0 + 8, :], ato[:], reads=[ato], writes=[at_d])
        P.dma("sp", qk_d[:, n0:n0 + 8, :], qo[:], reads=[qo], writes=[qk_d])
    maps = [{"x": qk2[i], "gcr": gcr[i], "ber": ber[i], "gcc": gcc[i], "ms": mstrict, "mi": mincl} for i in range(NCORE)]
    res = run_prog(P, maps)
    return [r["at"] for r in res], [r["qk"] for r in res]


def gdn_solve(At2):
    P = Prog()
    a_d = P.dram("a", [2, 128, 4096], F32, kind="ExternalInput")
    t_d = P.dram("t", [2, 128, 4096], F32, kind="ExternalOutput")
    A = [P.sbuf([128, 64, 64], F32, name=f"A{g}") for g in range(2)]
    T = [P.sbuf([128, 64, 64], F32, name=f"T{g}") for g in range(2)]
    tmp = [P.sbuf([128, 4096], F32, name=f"tmp{g}") for g in range(2)]
    for g in range(2):
        P.dma("sp", A[g][:].rearrange("p a b -> p (a b)"), a_d[g], reads=[a_d], writes=[A[g]])
        P.op("dve", lambda e, g=g: e.memset(T[g][:], 0.0), writes=[T[g]])
        P.op("dve", lambda e, g=g: e.memset(T[g][:].rearrange("p a b -> p (a b)")[:, 0:4096:65], 1.0), writes=[T[g]])
    for i in range(1, 64):
        for g in range(2):
            tv = tmp[g][:, 0:i * i].rearrange("p (c j) -> p c j", j=i)
            P.op("dve", lambda e, g=g, i=i, tv=tv: e.tensor_tensor(out=tv, in0=T[g][:, 0:i, 0:i],
                                                                 in1=A[g][:, 0:i, i:i + 1].rearrange("p j o -> p o j").broadcast_to([128, i, i]), op=ALU.mult),
                 reads=[T[g], A[g]], writes=[tmp[g]])
            P.op("dve", lambda e, g=g, i=i, tv=tv: e.tensor_reduce(out=T[g][:, 0:i, i], in_=tv, axis=AX.X, op=ALU.add, negate=True),
                 reads=[tmp[g]], writes=[T[g]])
    for g in range(2):
        P.dma("sp", t_d[g], T[g][:].rearrange("p a b -> p (a b)"), reads=[T[g]], writes=[t_d])
    res = run_prog(P, [{"a": At2[i]} for i in range(NCORE)])
    return [r["t"] for r in res]


def gdn_scan(Tt, vtok, ktok, ztok, bec, gcc, glc, qT, gcr128, glb128, qkT, onb):
    P = Prog()
    dI = lambda n, shp, dt=F32: P.dram(n, shp, dt, kind="ExternalInput")
    T_d = dI("T", [64, 256, 64]); v_d = dI("v", [64, 256, 128]); k_d = dI("k", [64, 256, 128]); z_d = dI("z", [64, 256, 128])
    bec_d = dI("bec", [64, 256]); gcc_d = dI("gcc", [64, 256]); glc_d = dI("glc", [64, 256])
    q_d = dI("q", [128, S]); gcr_d = dI("gcr", [128, S]); glb_d = dI("glb", [128, 256]); qk_d = dI("qk", [64, 256, 64]); on_d = dI("on", [64, 128])
    o_d = P.dram("o", [64, 256, 128], BF16, kind="ExternalOutput")

    def small(dr, shp, nm):
        b = P.sbuf(shp, F32, name=nm)
        P.dma("sp", b[:], dr[:], reads=[dr], writes=[b])
        return b
    bec_s = small(bec_d, [64, 256], "becs"); gcc_s = small(gcc_d, [64, 256], "gccs"); glc_s = small(glc_d, [64, 256], "glcs")
    egl = small(glb_d, [128, 256], "egl"); onb_s = small(on_d, [64, 128], "onbs")
    cb = P.sbuf([64, 256], F32, name="cb"); cd = P.sbuf([64, 256], F32, name="cd")
    eps1 = P.sbuf([64, 1], F32, name="eps1"); P.op("dve", lambda e: e.memset(eps1[:], EPS), writes=[eps1])
    P.op("act", lambda e: e.activation(out=cb[:], in_=gcc_s[:], func=AF.Exp), reads=[gcc_s], writes=[cb])
    P.op("dve", lambda e: e.tensor_tensor(out=cb[:], in0=cb[:], in1=bec_s[:], op=ALU.mult), reads=[cb, bec_s], writes=[cb])
    P.op("dve", lambda e: e.tensor_tensor(out=cd[:], in0=glc_s[:], in1=gcc_s[:], op=ALU.subtract), reads=[glc_s, gcc_s], writes=[cd])
    P.op("act", lambda e: e.activation(out=cd[:], in_=cd[:], func=AF.Exp), reads=[cd], writes=[cd])
    P.op("act", lambda e: e.activation(out=egl[:], in_=egl[:], func=AF.Exp), reads=[egl], writes=[egl])
    S32 = P.sbuf([128, 128], F32, name="S32"); Sb = P.sbuf([128, 128], BF16, name="Sb")
    P.op("dve", lambda e: e.memset(S32[:], 0.0), writes=[S32]); P.op("dve", lambda e: e.memset(Sb[:], 0.0), writes=[Sb])
    R = lambda nm, shp, dt=F32, n=2: Ring([P.sbuf(shp, dt, name=f"{nm}{i}") for i in range(n)])
    Tb_r = R("Tb", [64, 8, 64], BF16); vt_r = R("vt", [64, 8, 128]); kt_r = R("kt", [64, 8, 128]); zt_r = R("zt", [64, 8, 128])
    qkb_r = R("qkb", [64, 8, 64], BF16); qt_r = R("qt", [128, 512]); gt_r = R("gt", [128, 512])
    vb_r = R("vb", [64, 8, 128], BF16); kbe_r = R("kbe", [64, 8, 128], BF16); kd_r = R("kd", [64, 8, 128], BF16)
    qe_r = R("qe", [128, 512], BF16); u_r = R("u", [64, 8, 128]); wT_r = R("wT", [128, 8, 64], BF16); ot_r = R("ot", [64, 8, 128], BF16)
    vn_r = R("vn", [64, 128], BF16); jk_r = R("jk", [64, 128]); ss_r = R("ss", [64, 1], F32, 4)
    pu_r = Ring([P.psum([64, 128], F32, name=f"pu{i}") for i in range(2)])
    pw_r = Ring([P.psum([128, 64], F32, name=f"pw{i}") for i in range(1)])
    p1_r = Ring([P.psum([64, 128], F32, name=f"p1{i}") for i in range(2)])
    p2_r = Ring([P.psum([64, 128], F32, name=f"p2{i}") for i in range(2)])
    p3_r = Ring([P.psum([128, 128], F32, name=f"p3{i}") for i in range(1)])
    for t in range(S // 512):
        n0 = t * 8
        ns = slice(n0, n0 + 8)
        Tb = Tb_r.next(); vt = vt_r.next(); kt = kt_r.next(); zt = zt_r.next(); qkb = qkb_r.next(); qt = qt_r.next(); gt = gt_r.next()
        P.dma("pool", Tb[:], T_d[:, ns, :], reads=[T_d], writes=[Tb])
        P.dma("pool", qkb[:], qk_d[:, ns, :], reads=[qk_d], writes=[qkb])
        P.dma("sp", vt[:], v_d[:, ns, :], reads=[v_d], writes=[vt])
        P.dma("sp", kt[:], k_d[:, ns, :], reads=[k_d], writes=[kt])
        P.dma("sp", zt[:], z_d[:, ns, :], reads=[z_d], writes=[zt])
        P.dma("sp", qt[:], q_d[:, t * 512:(t + 1) * 512], reads=[q_d], writes=[qt])
        P.dma("sp", gt[:], gcr_d[:, t * 512:(t + 1) * 512], reads=[gcr_d], writes=[gt])
        vb = vb_r.next(); kbe = kbe_r.next(); kd = kd_r.next(); qe = qe_r.next()
        bc = lambda s_: s_[:, ns].unsqueeze(2).broadcast_to([64, 8, 128])
        b1, b2, b3 = bc(bec_s), bc(cb), bc(cd)
        P.op("pool", lambda e, vb=vb, vt=vt, b1=b1: e.tensor_tensor(out=vb[:], in0=vt[:], in1=b1, op=ALU.mult), reads=[vt, bec_s], writes=[vb])
        P.op("pool", lambda e, kbe=kbe, kt=kt, b2=b2: e.tensor_tensor(out=kbe[:], in0=kt[:], in1=b2, op=ALU.mult), reads=[kt, cb], writes=[kbe])
        P.op("pool", lambda e, kd=kd, kt=kt, b3=b3: e.tensor_tensor(out=kd[:], in0=kt[:], in1=b3, op=ALU.mult), reads=[kt, cd], writes=[kd])
        P.op("act", lambda e, gt=gt: e.activation(out=gt[:], in_=gt[:], func=AF.Exp), reads=[gt], writes=[gt])
        P.op("pool", lambda e, qe=qe, qt=qt, gt=gt: e.tensor_tensor(out=qe[:], in0=qt[:], in1=gt[:], op=ALU.mult), reads=[qt, gt], writes=[qe])
        P.op("act", lambda e, zt=zt: e.activation(out=zt[:], in_=zt[:], func=AF.Silu), reads=[zt], writes=[zt])
        P.op("pool", lambda e, zt=zt: e.tensor_tensor(out=zt[:], in0=zt[:], in1=onb_s[:].unsqueeze(1).broadcast_to([64, 8, 128]), op=ALU.mult), reads=[zt, onb_s], writes=[zt])
        u = u_r.next(); wT = wT_r.next(); ot = ot_r.next()
        for n in range(8):
            pu = pu_r.next(); pw = pw_r.next()
            P.op("pe", lambda e, pu=pu, Tb=Tb, vb=vb, n=n: e.matmul(pu[:], lhsT=Tb[:, n, :], rhs=vb[:, n, :], start=True, stop=True), reads=[Tb, vb], writes=[pu])
            P.op("pe", lambda e, pw=pw, Tb=Tb, kbe=kbe, n=n: e.matmul(pw[:], lhsT=kbe[:, n, :], rhs=Tb[:, n, :], start=True, stop=True), reads=[Tb, kbe], writes=[pw])
            P.op("act", lambda e, pu=pu, u=u, n=n: e.copy(out=u[:, n, :], in_=pu[:]), reads=[pu], writes=[u])
            P.op("act", lambda e, pw=pw, wT=wT, n=n: e.copy(out=wT[:, n, :], in_=pw[:]), reads=[pw], writes=[wT])
        for n in range(8):
            p1 = p1_r.next(); p2 = p2_r.next(); p3 = p3_r.next(); vn = vn_r.next()
            P.op("pe", lambda e, p1=p1, wT=wT, n=n: e.matmul(p1[:], lhsT=wT[:, n, :], rhs=Sb[:], start=True, stop=True), reads=[wT, Sb], writes=[p1])
            P.op("dve", lambda e, vn=vn, u=u, p1=p1, n=n: e.tensor_tensor(out=vn[:], in0=u[:, n, :], in1=p1[:], op=ALU.subtract), reads=[u, p1], writes=[vn])
            P.op("pe", lambda e, p2=p2, qe=qe, n=n: e.matmul(p2[:], lhsT=qe[:, n * 64:(n + 1) * 64], rhs=Sb[:], start=True, stop=False), reads=[qe, Sb], writes=[p2])
            P.op("pe", lambda e, p2=p2, qkb=qkb, vn=vn, n=n: e.matmul(p2[:], lhsT=qkb[:, n, :], rhs=vn[:], start=False, stop=True), reads=[qkb, vn], writes=[p2])
            P.op("pe", lambda e, p3=p3, kd=kd, vn=vn, n=n: e.matmul(p3[:], lhsT=kd[:, n, :], rhs=vn[:], start=True, stop=True), reads=[kd, vn], writes=[p3])
            P.op("dve", lambda e, p3=p3, n0=n0, n=n: e.scalar_tensor_tensor(out=S32[:], in0=S32[:], scalar=egl[:, n0 + n:n0 + n + 1], in1=p3[:], op0=ALU.mult, op1=ALU.add),
                 reads=[S32, egl, p3], writes=[S32])
            P.op("act", lambda e: e.copy(out=Sb[:], in_=S32[:]), reads=[S32], writes=[Sb])
            jk = jk_r.next(); ss = ss_r.next()
            P.op("act", lambda e, jk=jk, p2=p2, ss=ss: e.activation(out=jk[:], in_=p2[:], func=AF.Square, accum_out=ss[:, 0:1]), reads=[p2], writes=[jk, ss])
            P.op("act", lambda e, ss=ss: e.activation(out=ss[:], in_=ss[:], func=AF.Sqrt, bias=eps1[:, 0:1], scale=1.0 / 128), reads=[ss, eps1], writes=[ss])
            P.op("dve", lambda e, ss=ss: e.reciprocal(out=ss[:], in_=ss[:]), reads=[ss], writes=[ss])
            P.op("dve", lambda e, ot=ot, p2=p2, ss=ss, zt=zt, n=n: e.scalar_tensor_tensor(out=ot[:, n, :], in0=p2[:], scalar=ss[:, 0:1], in1=zt[:, n, :], op0=ALU.mult, op1=ALU.mult),
                 reads=[p2, ss, zt], writes=[ot])
        P.dma("sp", o_d[:, ns, :], ot[:], reads=[ot], writes=[o_d])
    maps = [{"T": Tt[i], "v": vtok[i], "k": ktok[i], "z": ztok[i], "bec": bec[i], "gcc": gcc[i], "glc": glc[i], "q": qT[i], "gcr": gcr128[i],
             "glb": glb128[i], "qk": qkT[i], "on": onb} for i in range(NCORE)]
    res = run_prog(P, maps)
    return [r["o"] for r in res]


def layer1_mixer(h_allT, inp):
    c = np.ascontiguousarray
    w_in = inp["dn_w_in"][0]
    wc = [c(np.concatenate([w_in[:, o + 128 * i:o + 128 * (i + 1)] for o in (0, 1024, 2048, 3072)] + [w_in[:, 4096 + i:4097 + i], w_in[:, 4104 + i:4105 + i]], axis=1))
          for i in range(NCORE)]
    proj = head_proj(h_allT, wc, 514)
    cwT = inp["dn_conv_w"][0].T
    qkv_pad, cw3, a128, b128, scal = [], [], [], [], []
    for i, p in enumerate(proj):
        x3 = p[0:384].reshape(3, 128, S)
        qkv_pad.append(c(np.concatenate([np.zeros((3, 128, 3), np.float32), x3], axis=2)))
        cw3.append(c(np.concatenate([cwT[o + 128 * i:o + 128 * (i + 1)] for o in (0, 1024, 2048)], axis=1)))
        a128.append(c(p[512].reshape(128, 128))); b128.append(c(p[513].reshape(128, 128)))
        scal.append(c(np.broadcast_to(np.array([inp["dn_a_log"][0][i], inp["dn_dt_bias"][0][i]], np.float32)[None, :], (128, 2))))
    qkv, gc, be = gdn_prep(qkv_pad, cw3, a128, b128, scal)
    gcrow = [g.reshape(S) for g in gc]; berow = [b.reshape(S) for b in be]
    col = lambda r: c(r.reshape(256, 64).T)
    gcc = [col(r) for r in gcrow]; bec = [col(r) for r in berow]
    glc = [c(np.broadcast_to(r.reshape(256, 64)[:, 63][None, :], (64, 256))) for r in gcrow]
    jj = np.arange(64)[:, None]; ii = np.arange(64)[None, :]
    mstrict = np.where(jj < ii, 0.0, NEG).astype(np.float32); mincl = np.where(jj <= ii, 0.0, NEG).astype(np.float32)
    At, qkT = gdn_amat([c(x[0:2]) for x in qkv], [c(np.broadcast_to(r[None, :], (64, S))) for r in gcrow],
                       [c(np.broadcast_to(r[None, :], (64, S))) for r in berow], gcc, mstrict, mincl)
    At2 = [c(a.transpose(1, 0, 2).reshape(2, 128, 4096)) for a in At]
    Tt = gdn_solve(At2)
    Ttj = [c(t_.reshape(256, 64, 64).transpose(1, 0, 2)) for t_ in Tt]
    tok = lambda xT: c(xT.reshape(128, 256, 64).transpose(2, 1, 0))
    vtok = [tok(x[2]) for x in qkv]; ktok = [tok(x[1]) for x in qkv]; ztok = [tok(p[384:512]) for p in proj]
    gcr128 = [c(np.broadcast_to(r[None, :], (128, S))) for r in gcrow]
    glb128 = [c(np.broadcast_to(r.reshape(256, 64)[:, 63][None, :], (128, 256))) for r in gcrow]
    onb = c(np.broadcast_to(inp["dn_o_norm"][0][None, :], (64, 128)).astype(np.float32))
    o = gdn_scan(Ttj, vtok, ktok, ztok, bec, gcc, glc, [c(x[0]) for x in qkv], gcr128, glb128, qkT, onb)
    oT = [c(x.transpose(1, 0, 2).reshape(S, 128).T) for x in o]
    return np.concatenate(oT, axis=0)


def phase_F(xT_sh, g_final):
    P = Prog()
    x_d = P.dram("x", [D, TPC], F32, kind="ExternalInput")
    g_d = P.dram("gfin", [128, 8], F32, kind="ExternalInput")
    yo_d = P.dram("yo", [D, TPC], F32, kind="ExternalOutput")
    C = TokCtx(P)
    C.eps_ap()
    gfin = C.load_small(g_d, [128, 8], "gfin")
    C.load_x(x_d)
    for t in range(TPC // 512):
        ps = C.ps_stat
        for c in range(8):
            sq = C.sq.next()
            x = C.xT[c][t]
            P.op("act", lambda e, sq=sq, x=x: e.activation(out=sq[:], in_=x[:], func=AF.Square), reads=[x], writes=[sq])
            P.op("pe", lambda e, sq=sq, c=c: e.matmul(ps[:], lhsT=C.ones[:], rhs=sq[:], start=(c == 0), stop=(c == 7)),
                 reads=[sq, C.ones], writes=[ps])
        P.op("act", lambda e: e.activation(out=C.rstd[:], in_=ps[:], func=AF.Sqrt, scale=1.0 / D, bias=C.eps_ap()),
             reads=[ps, C.epsb], writes=[C.rstd])
        P.op("dve", lambda e: e.reciprocal(out=C.rstd[:], in_=C.rstd[:]), reads=[C.rstd], writes=[C.rstd])
        for c in range(8):
            x = C.xT[c][t]
            P.op("dve", lambda e, c=c, x=x: e.scalar_tensor_tensor(out=x[:], in0=x[:], scalar=gfin[:, c:c + 1], in1=C.rstd[:],
                                                                    op0=ALU.mult, op1=ALU.mult),
                 reads=[x, gfin, C.rstd], writes=[x])
    C.store_x(yo_d)
    res = run_prog(P, [{"x": xT_sh[i], "gfin": vec128(g_final)} for i in range(NCORE)])
    return [r["yo"] for r in res]


def kernel(**inp):
    inp = {k: np.asarray(v) for k, v in inp.items()}
    c = np.ascontiguousarray
    sh = lambda xT: [c(xT[:, i * TPC:(i + 1) * TPC]) for i in range(NCORE)]
    xT = c(inp["x"][0].T)
    memT = c(inp["mem"][0].T)
    L = lambda n, l: c(inp[n][l])
    x1, h = phase_A(sh(xT), inp["ffn1_norm"][0], L("ffn1_w_in", 0), L("ffn1_w_out", 0), inp["mix_norm"][0])
    mix = layer0_mixer(c(np.concatenate(h, axis=1)), inp)
    x4 = phase_C(x1, sh(mix), L("ab_w_out", 0), inp["xa_norm"][0], inp["xa_mem_norm"][0], memT, L("xa_wq", 0), L("xa_wkv", 0), L("xa_wo", 0),
                 inp["ffn2_norm"][0], L("ffn2_w_in", 0), L("ffn2_w_out", 0), plain=True)
    x5, h = phase_A(x4, inp["ffn1_norm"][1], L("ffn1_w_in", 1), L("ffn1_w_out", 1), inp["mix_norm"][1])
    mix = layer1_mixer(c(np.concatenate(h, axis=1)), inp)
    x8 = phase_C(x5, sh(mix), L("dn_w_out", 0), inp["xa_norm"][1], inp["xa_mem_norm"][1], memT, L("xa_wq", 1), L("xa_wkv", 1), L("xa_wo", 1),
                 inp["ffn2_norm"][1], L("ffn2_w_in", 1), L("ffn2_w_out", 1), plain=True)
    y = phase_F(x8, inp["final_norm"])
    out = np.concatenate(y, axis=1).T
    return np.ascontiguousarray(out[None]).astype(np.float32)
```

```python
import numpy as np
import ml_dtypes
import concourse.bass as bass
import concourse.mybir as mybir
from concourse.bass_utils import run_bass_kernel_spmd

F32 = mybir.dt.float32
BF16 = mybir.dt.bfloat16
AF = mybir.ActivationFunctionType
ALU = mybir.AluOpType
AX = mybir.AxisListType
NPBF = ml_dtypes.bfloat16

NCORE = 8
D = 1024
S = 16384
TPC = S // NCORE
DFF = 2816
NFF = DFF // 128
EPS = 1e-6
NMEM = 256
NEG = -30000.0

NDMA_SEMS = 8


class Buf:
    def __init__(self, t, name=""):
        self.t = t
        self.name = name
        self.last_write = None
        self.reads = []

    def __getitem__(self, idx):
        return self.t[idx]


class Prog:
    ENGS = ("pe", "act", "dve", "pool", "sp")

    def __init__(self):
        self.nc = bass.Bass("TRN2", target_bir_lowering=False)
        nc = self.nc
        self.ops = {e: [] for e in self.ENGS}
        self.sem = {e: nc.alloc_semaphore("s_" + e) for e in self.ENGS}
        self.count = {e: 0 for e in self.ENGS}
        self.dsem = {e: [nc.alloc_semaphore(f"d_{e}{i}") for i in range(NDMA_SEMS)] for e in ("sp", "act", "pool")}
        self.dcount = {e: 0 for e in ("sp", "act", "pool")}
        self.waited = {e: {} for e in self.ENGS}
        self.nbuf = 0
        self.outs = []

    def sbuf(self, shape, dtype=F32, name=None):
        self.nbuf += 1
        name = name or f"sb{self.nbuf}"
        return Buf(self.nc.alloc_sbuf_tensor(name, list(shape), dtype), name)

    def psum(self, shape, dtype=F32, name=None):
        self.nbuf += 1
        name = name or f"ps{self.nbuf}"
        return Buf(self.nc.alloc_psum_tensor(name, list(shape), dtype), name)

    def dram(self, name, shape, dtype=F32, kind="Internal"):
        b = Buf(self.nc.dram_tensor(name, list(shape), dtype, kind=kind), name)
        if kind == "ExternalOutput":
            self.outs.append(b)
        return b

    def _deps(self, e, reads, writes):
        deps = set()
        for b in reads:
            if b.last_write is not None:
                deps.add(b.last_write)
        for b in writes:
            if b.last_write is not None:
                deps.add(b.last_write)
            deps.update(b.reads)
        waits = []
        for key, val in deps:
            if key == e and e == "pe":
                continue
            if self.waited[e].get(key, 0) >= val:
                continue
            self.waited[e][key] = val
            waits.append((key, val))
        return waits

    def _semof(self, key):
        if isinstance(key, tuple):
            return self.dsem[key[1]][key[2]]
        return self.sem[key]

    def op(self, e, fn, reads=(), writes=()):
        waits = self._deps(e, reads, writes)
        self.count[e] += 1
        seq = self.count[e]
        sem = self.sem[e]
        wl = [(self._semof(k), v) for k, v in waits]

        def emit(eng):
            for s, v in wl:
                eng.wait_ge(s, v)
            fn(eng).then_inc(sem, 1)

        self.ops[e].append(emit)
        ev = (e, seq)
        for b in reads:
            b.reads.append(ev)
        for b in writes:
            b.last_write = ev
            b.reads = []
        return ev

    def dma(self, q, out_ap, in_ap, reads=(), writes=(), **kw):
        j = self.dcount[q]
        self.dcount[q] += 1
        slot = j % NDMA_SEMS
        rnd = j // NDMA_SEMS
        key = ("d", q, slot)
        waits = self._deps(q, reads, writes)
        if rnd > 0 and self.waited[q].get(key, 0) < 16 * rnd:
            self.waited[q][key] = 16 * rnd
            waits.append((key, 16 * rnd))
        wl = [(self._semof(k), v) for k, v in waits]
        sem = self.dsem[q][slot]

        def emit(eng):
            for s, v in wl:
                eng.wait_ge(s, v)
            eng.dma_start(out=out_ap, in_=in_ap, **kw).then_inc(sem, 16)

        self.ops[q].append(emit)
        ev = (key, 16 * (rnd + 1))
        for b in reads:
            b.reads.append(ev)
        for b in writes:
            b.last_write = ev
            b.reads = []
        return ev

    def build(self):
        waits = self._deps("sp", self.outs, ())
        wl = [(self._semof(k), v) for k, v in waits]
        self.ops["sp"].append(lambda eng: [eng.wait_ge(s, v) for s, v in wl])
        nc = self.nc
        ops = self.ops
        with nc.Block() as block:
            @block.tensor
            def _(eng):
                for f in ops["pe"]:
                    f(eng)

            @block.scalar
            def _(eng):
                for f in ops["act"]:
                    f(eng)

            @block.vector
            def _(eng):
                for f in ops["dve"]:
                    f(eng)

            @block.gpsimd
            def _(eng):
                for f in ops["pool"]:
                    f(eng)

            @block.sync
            def _(eng):
                for f in ops["sp"]:
                    f(eng)
        return nc


class Ring:
    def __init__(self, bufs):
        self.bufs = bufs
        self.i = 0

    def next(self):
        b = self.bufs[self.i % len(self.bufs)]
        self.i += 1
        return b


def vec128(v):
    v = np.asarray(v)
    return np.ascontiguousarray(v.reshape(-1, 128).T)


def run_prog(P, in_maps):
    P.build()
    res = run_bass_kernel_spmd(P.nc, in_maps, core_ids=list(range(NCORE)))
    return res.results


class TokCtx:
    def __init__(self, P, ntok=TPC):
        self.P = P
        self.ntok = ntok
        self.xT = [[P.sbuf([128, 512], F32, name=f"xT{c}_{t}") for t in range(ntok // 512)] for c in range(8)]
        self.ones = P.sbuf([128, 128], BF16, name="ones")
        P.op("dve", lambda e: e.memset(self.ones[:], 1.0), writes=[self.ones])
        self.hT = [P.sbuf([128, 512], BF16, name=f"hT{c}") for c in range(8)]
        self.sq = Ring([P.sbuf([128, 512], BF16, name=f"sq{i}") for i in range(3)])
        self.rstd = P.sbuf([128, 512], F32, name="rstd")
        self.ps_stat = P.psum([128, 512], F32, name="ps_stat")
        self.ps_a = Ring([P.psum([128, 512], F32, name=f"ps_a{i}") for i in range(2)])
        self.ps_b = Ring([P.psum([128, 512], F32, name=f"ps_b{i}") for i in range(2)])
        self.ps_o = Ring([P.psum([128, 512], F32, name=f"ps_o{i}") for i in range(2)])

    def load_small(self, dram_buf, shape, name, dtype=F32):
        P = self.P
        g = P.sbuf(list(shape), dtype, name=name + "_sb")
        P.dma("sp", g[:], dram_buf[:], reads=[dram_buf], writes=[g])
        return g

    def rmsnorm(self, t, g, out=None, ntok=512, xsrc=None):
        P = self.P
        out = out or self.hT
        xs = xsrc or [self.xT[c][t] for c in range(8)]
        ps = self.ps_stat
        for c in range(8):
            sq = self.sq.next()
            P.op("act", lambda e, sq=sq, x=xs[c]: e.activation(out=sq[:, :ntok], in_=x[:, :ntok], func=AF.Square),
                 reads=[xs[c]], writes=[sq])
            P.op("pe", lambda e, sq=sq, c=c: e.matmul(ps[:, :ntok], lhsT=self.ones[:], rhs=sq[:, :ntok], start=(c == 0), stop=(c == 7)),
                 reads=[sq, self.ones], writes=[ps])
        rstd = self.rstd
        P.op("act", lambda e: e.activation(out=rstd[:, :ntok], in_=ps[:, :ntok], func=AF.Sqrt, scale=1.0 / D, bias=self.eps_ap()),
             reads=[ps, self.epsb], writes=[rstd])
        P.op("dve", lambda e: e.reciprocal(out=rstd[:, :ntok], in_=rstd[:, :ntok]), reads=[rstd], writes=[rstd])
        for c in range(8):
            P.op("dve", lambda e, c=c: e.scalar_tensor_tensor(out=out[c][:, :ntok], in0=xs[c][:, :ntok], scalar=g[:, c:c + 1],
                                                               in1=rstd[:, :ntok], op0=ALU.mult, op1=ALU.mult),
                 reads=[xs[c], g, rstd], writes=[out[c]])
        return out

    def eps_ap(self):
        if not hasattr(self, "epsb"):
            self.epsb = self.P.sbuf([128, 1], F32, name="epsb")
            self.P.op("dve", lambda e: e.memset(self.epsb[:], EPS), writes=[self.epsb])
        return self.epsb[:, 0:1]

    def ffn(self, t, g, w_in, w_out, st):
        P = self.P
        hT = self.rmsnorm(t, g)
        win_v = w_in.t.ap().rearrange("(kc p) n -> p kc n", p=128)
        wout_v = w_out.t.ap().rearrange("(j p) n -> p j n", p=128)
        actT = st["actT"]
        for j2 in range(NFF // 2):
            wg = st["wg"].next()
            wu = st["wu"].next()
            P.dma("pool", wg[:], win_v[:, :, j2 * 256:(j2 + 1) * 256], reads=[w_in], writes=[wg])
            P.dma("pool", wu[:], win_v[:, :, DFF + j2 * 256:DFF + (j2 + 1) * 256], reads=[w_in], writes=[wu])
            for jj in range(2):
                j = 2 * j2 + jj
                pg = self.ps_a.next()
                pu = self.ps_b.next()
                for kc in range(8):
                    P.op("pe", lambda e, pg=pg, wg=wg, kc=kc, jj=jj: e.matmul(pg[:], lhsT=wg[:, kc, jj * 128:(jj + 1) * 128], rhs=hT[kc][:],
                                                                             start=(kc == 0), stop=(kc == 7)),
                         reads=[wg, hT[kc]], writes=[pg])
                for kc in range(8):
                    P.op("pe", lambda e, pu=pu, wu=wu, kc=kc, jj=jj: e.matmul(pu[:], lhsT=wu[:, kc, jj * 128:(jj + 1) * 128], rhs=hT[kc][:],
                                                                             start=(kc == 0), stop=(kc == 7)),
                         reads=[wu, hT[kc]], writes=[pu])
                sg = st["sg"].next()
                P.op("act", lambda e, sg=sg, pg=pg: e.activation(out=sg[:], in_=pg[:], func=AF.Silu), reads=[pg], writes=[sg])
                P.op("dve", lambda e, sg=sg, pu=pu, j=j: e.tensor_tensor(out=actT[j][:], in0=sg[:], in1=pu[:], op=ALU.mult),
                     reads=[sg, pu], writes=[actT[j]])
        for dc in range(8):
            wo = st["wo"].next()
            P.dma("pool", wo[:], wout_v[:, :, dc * 128:(dc + 1) * 128], reads=[w_out], writes=[wo])
            po = self.ps_o.next()
            for j in range(NFF):
                P.op("pe", lambda e, po=po, wo=wo, j=j: e.matmul(po[:], lhsT=wo[:, j, :], rhs=actT[j][:],
                                                                 start=(j == 0), stop=(j == NFF - 1)),
                     reads=[wo, actT[j]], writes=[po])
            x = self.xT[dc][t]
            P.op("dve", lambda e, po=po, x=x: e.scalar_tensor_tensor(out=x[:], in0=po[:], scalar=0.5, in1=x[:], op0=ALU.mult, op1=ALU.add),
                 reads=[po, x], writes=[x])

    def proj_add(self, t, w_sb, srcT):
        P = self.P
        for dc in range(8):
            po = self.ps_o.next()
            for kc in range(8):
                P.op("pe", lambda e, po=po, kc=kc, dc=dc: e.matmul(po[:], lhsT=w_sb[:, kc, dc * 128:(dc + 1) * 128], rhs=srcT[kc][:],
                                                                   start=(kc == 0), stop=(kc == 7)),
                     reads=[w_sb, srcT[kc]], writes=[po])
            x = self.xT[dc][t]
            P.op("dve", lambda e, po=po, x=x: e.tensor_tensor(out=x[:], in0=po[:], in1=x[:], op=ALU.add), reads=[po, x], writes=[x])

    def load_w_bf16(self, w_sb, w_dram, ncols=1024, c0=0):
        P = self.P
        v = w_dram.t.ap().rearrange("(kc p) n -> p kc n", p=128)
        for s0 in range(0, ncols, 256):
            P.dma("pool", w_sb[:, :, s0:s0 + 256], v[:, :, c0 + s0:c0 + s0 + 256], reads=[w_dram], writes=[w_sb])

    def ffn_state(self):
        P = self.P
        return {
            "actT": [P.sbuf([128, 512], BF16, name=f"actT{j}") for j in range(NFF)],
            "wg": Ring([P.sbuf([128, 8, 256], BF16, name=f"wg{i}") for i in range(2)]),
            "wu": Ring([P.sbuf([128, 8, 256], BF16, name=f"wu{i}") for i in range(2)]),
            "wo": Ring([P.sbuf([128, NFF, 128], BF16, name=f"wo{i}") for i in range(3)]),
            "sg": Ring([P.sbuf([128, 512], F32, name=f"sg{i}") for i in range(2)]),
        }

    def load_x(self, x_dram):
        P = self.P
        v = x_dram.t.ap().rearrange("(c p) n -> p c n", p=128)
        for c in range(8):
            for t in range(self.ntok // 512):
                P.dma("sp", self.xT[c][t][:], v[:, c, t * 512:(t + 1) * 512], reads=[x_dram], writes=[self.xT[c][t]])

    def store_x(self, x_dram):
        P = self.P
        v = x_dram.t.ap().rearrange("(c p) n -> p c n", p=128)
        for c in range(8):
            for t in range(self.ntok // 512):
                P.dma("sp", v[:, c, t * 512:(t + 1) * 512], self.xT[c][t][:], reads=[self.xT[c][t]], writes=[x_dram])


def phase_A(xT_sh, g1, w_in, w_out, gmix):
    P = Prog()
    x_d = P.dram("x", [D, TPC], F32, kind="ExternalInput")
    g1_d = P.dram("g1", [128, 8], F32, kind="ExternalInput")
    gm_d = P.dram("gm", [128, 8], F32, kind="ExternalInput")
    wi_d = P.dram("wi", [D, 2 * DFF], F32, kind="ExternalInput")
    wo_d = P.dram("wo", [DFF, D], F32, kind="ExternalInput")
    xo_d = P.dram("xo", [D, TPC], F32, kind="ExternalOutput")
    ho_d = P.dram("ho", [D, TPC], BF16, kind="ExternalOutput")
    C = TokCtx(P)
    C.eps_ap()
    g1s = C.load_small(g1_d, [128, 8], "g1s")
    gms = C.load_small(gm_d, [128, 8], "gms")
    C.load_x(x_d)
    st = C.ffn_state()
    hov = ho_d.t.ap().rearrange("(c p) n -> p c n", p=128)
    for t in range(TPC // 512):
        C.ffn(t, g1s, wi_d, wo_d, st)
        hT = C.rmsnorm(t, gms)
        for c in range(8):
            P.dma("sp", hov[:, c, t * 512:(t + 1) * 512], hT[c][:], reads=[hT[c]], writes=[ho_d])
    C.store_x(xo_d)
    maps = [{"x": xT_sh[i], "g1": vec128(g1), "gm": vec128(gmix), "wi": w_in, "wo": w_out} for i in range(NCORE)]
    res = run_prog(P, maps)
    return [r["xo"] for r in res], [r["ho"] for r in res]


def phase_C(xT_sh, mixT_sh, w_mo, g_xa, g_mem, memT, wq, wkv, wo, g_f2, w2_in, w2_out, nxt=None, g_final=None, plain=False):
    P = Prog()
    dI = lambda n, shp, dt=F32: P.dram(n, shp, dt, kind="ExternalInput")
    x_d = dI("x", [D, TPC]); mix_d = dI("mix", [D, TPC], BF16); wmo_d = dI("wmo", [D, D])
    gxa_d = dI("gxa", [128, 8]); gmem_d = dI("gmem", [128, 8]); mem_d = dI("mem", [D, NMEM])
    wq_d = dI("wq", [D, D]); wkv_d = dI("wkv", [D, 2 * D]); wo_d = dI("wo", [D, D])
    gf2_d = dI("gf2", [128, 8]); w2i_d = dI("w2i", [D, 2 * DFF]); w2o_d = dI("w2o", [DFF, D])
    if nxt is not None:
        gn1_d = dI("gn1", [128, 8]); wn1i_d = dI("wn1i", [D, 2 * DFF]); wn1o_d = dI("wn1o", [DFF, D]); gnm_d = dI("gnm", [128, 8])
        xo_d = P.dram("xo", [D, TPC], F32, kind="ExternalOutput")
        ho_d = P.dram("ho", [D, TPC], BF16, kind="ExternalOutput")
    elif plain:
        yo_d = P.dram("yo", [D, TPC], F32, kind="ExternalOutput")
    else:
        gfin_d = dI("gfin", [128, 8])
        yo_d = P.dram("yo", [D, TPC], F32, kind="ExternalOutput")
    C = TokCtx(P)
    C.eps_ap()
    gxa = C.load_small(gxa_d, [128, 8], "gxa"); gmem = C.load_small(gmem_d, [128, 8], "gmem"); gf2 = C.load_small(gf2_d, [128, 8], "gf2")
    if nxt is not None:
        gn1 = C.load_small(gn1_d, [128, 8], "gn1"); gnm = C.load_small(gnm_d, [128, 8], "gnm")
    elif not plain:
        gfin = C.load_small(gfin_d, [128, 8], "gfin")
    C.load_x(x_d)
    st = C.ffn_state()
    actT = st["actT"]
    W1 = P.sbuf([128, 8, 1024], BF16, name="W1")
    W2 = P.sbuf([128, 8, 1024], BF16, name="W2")
    NT = TPC // 512
    C.load_w_bf16(W1, wmo_d)
    mixv = mix_d.t.ap().rearrange("(c p) n -> p c n", p=128)
    for t in range(NT):
        src = actT[8:16]
        for c in range(8):
            P.dma("sp", src[c][:], mixv[:, c, t * 512:(t + 1) * 512], reads=[mix_d], writes=[src[c]])
        C.proj_add(t, W1, src)
    memx = [P.sbuf([128, NMEM], F32, name=f"memx{c}") for c in range(8)]
    mhT = [P.sbuf([128, NMEM], BF16, name=f"mhT{c}") for c in range(8)]
    kT = [P.sbuf([128, NMEM], BF16, name=f"kT{c}") for c in range(8)]
    Vm = [P.sbuf([128, D], BF16, name=f"Vm{c}") for c in range(2)]
    memv = mem_d.t.ap().rearrange("(c p) n -> p c n", p=128)
    for c in range(8):
        P.dma("sp", memx[c][:], memv[:, c, :], reads=[mem_d], writes=[memx[c]])
    C.rmsnorm(0, gmem, out=mhT, ntok=NMEM, xsrc=memx)
    wkv_v = wkv_d.t.ap().rearrange("(kc p) n -> p kc n", p=128)
    for s8 in range(4):
        wsl = st["wg"].next()
        P.dma("pool", wsl[:], wkv_v[:, :, s8 * 256:(s8 + 1) * 256], reads=[wkv_d], writes=[wsl])
        for jj in range(2):
            fc = 2 * s8 + jj
            ps = C.ps_a.next()
            for kc in range(8):
                P.op("pe", lambda e, ps=ps, wsl=wsl, kc=kc, jj=jj: e.matmul(ps[:, :NMEM], lhsT=wsl[:, kc, jj * 128:(jj + 1) * 128], rhs=mhT[kc][:],
                                                                           start=(kc == 0), stop=(kc == 7)),
                     reads=[wsl, mhT[kc]], writes=[ps])
            P.op("act", lambda e, ps=ps, fc=fc: e.copy(out=kT[fc][:], in_=ps[:, :NMEM]), reads=[ps], writes=[kT[fc]])
    for s8 in range(4):
        wsl = st["wu"].next()
        P.dma("pool", wsl[:], wkv_v[:, :, D + s8 * 256:D + (s8 + 1) * 256], reads=[wkv_d], writes=[wsl])
        for mc in range(2):
            ps = C.ps_b.next()
            for kc in range(8):
                P.op("pe", lambda e, ps=ps, wsl=wsl, kc=kc, mc=mc: e.matmul(ps[:, :256], lhsT=mhT[kc][:, mc * 128:(mc + 1) * 128], rhs=wsl[:, kc, :],
                                                                           start=(kc == 0), stop=(kc == 7)),
                     reads=[wsl, mhT[kc]], writes=[ps])
            P.op("act", lambda e, ps=ps, mc=mc, s8=s8: e.copy(out=Vm[mc][:, s8 * 256:(s8 + 1) * 256], in_=ps[:, :256]), reads=[ps], writes=[Vm[mc]])
    C.load_w_bf16(W2, wq_d)
    C.load_w_bf16(W1, wo_d)
    pT = [P.sbuf([128, 512], BF16, name=f"pT{i}") for i in range(2)]
    rden = P.sbuf([128, 512], F32, name="rden")
    qT = actT[0:8]
    oT = actT[8:16]
    XS = (D // 4) ** -0.5
    for t in range(NT):
        hT = C.rmsnorm(t, gxa)
        for fc in range(8):
            ps = C.ps_a.next()
            for kc in range(8):
                P.op("pe", lambda e, ps=ps, kc=kc, fc=fc: e.matmul(ps[:], lhsT=W2[:, kc, fc * 128:(fc + 1) * 128], rhs=hT[kc][:],
                                                                   start=(kc == 0), stop=(kc == 7)),
                     reads=[W2, hT[kc]], writes=[ps])
            P.op("act", lambda e, ps=ps, fc=fc: e.copy(out=qT[fc][:], in_=ps[:]), reads=[ps], writes=[qT[fc]])
        for hh in range(4):
            fcs = (2 * hh, 2 * hh + 1)
            for mc in range(2):
                ps = C.ps_a.next()
                for i, fc in enumerate(fcs):
                    P.op("pe", lambda e, ps=ps, fc=fc, mc=mc, i=i: e.matmul(ps[:], lhsT=kT[fc][:, mc * 128:(mc + 1) * 128], rhs=qT[fc][:],
                                                                           start=(i == 0), stop=(i == 1)),
                         reads=[kT[fc], qT[fc]], writes=[ps])
                P.op("act", lambda e, ps=ps, mc=mc: e.activation(out=pT[mc][:], in_=ps[:], func=AF.Exp, scale=XS), reads=[ps], writes=[pT[mc]])
            pd = C.ps_stat
            for mc in range(2):
                P.op("pe", lambda e, mc=mc: e.matmul(pd[:], lhsT=C.ones[:], rhs=pT[mc][:], start=(mc == 0), stop=(mc == 1)),
                     reads=[C.ones, pT[mc]], writes=[pd])
            P.op("dve", lambda e: e.reciprocal(out=rden[:], in_=pd[:]), reads=[pd], writes=[rden])
            for fc in fcs:
                ps = C.ps_b.next()
                for mc in range(2):
                    P.op("pe", lambda e, ps=ps, fc=fc, mc=mc: e.matmul(ps[:], lhsT=Vm[mc][:, fc * 128:(fc + 1) * 128], rhs=pT[mc][:],
                                                                       start=(mc == 0), stop=(mc == 1)),
                         reads=[Vm[mc], pT[mc]], writes=[ps])
                P.op("dve", lambda e, ps=ps, fc=fc: e.tensor_tensor(out=oT[fc][:], in0=ps[:], in1=rden[:], op=ALU.mult),
                     reads=[ps, rden], writes=[oT[fc]])
        C.proj_add(t, W1, oT)
    for t in range(NT):
        C.ffn(t, gf2, w2i_d, w2o_d, st)
        if nxt is not None:
            C.ffn(t, gn1, wn1i_d, wn1o_d, st)
            hT = C.rmsnorm(t, gnm)
            hov = ho_d.t.ap().rearrange("(c p) n -> p c n", p=128)
            for c in range(8):
                P.dma("sp", hov[:, c, t * 512:(t + 1) * 512], hT[c][:], reads=[hT[c]], writes=[ho_d])
        elif not plain:
            ps = C.ps_stat
            for c in range(8):
                sq = C.sq.next()
                x = C.xT[c][t]
                P.op("act", lambda e, sq=sq, x=x: e.activation(out=sq[:], in_=x[:], func=AF.Square), reads=[x], writes=[sq])
                P.op("pe", lambda e, sq=sq, c=c: e.matmul(ps[:], lhsT=C.ones[:], rhs=sq[:], start=(c == 0), stop=(c == 7)),
                     reads=[sq, C.ones], writes=[ps])
            P.op("act", lambda e: e.activation(out=C.rstd[:], in_=ps[:], func=AF.Sqrt, scale=1.0 / D, bias=C.eps_ap()),
                 reads=[ps, C.epsb], writes=[C.rstd])
            P.op("dve", lambda e: e.reciprocal(out=C.rstd[:], in_=C.rstd[:]), reads=[C.rstd], writes=[C.rstd])
            for c in range(8):
                x = C.xT[c][t]
                P.op("dve", lambda e, c=c, x=x: e.scalar_tensor_tensor(out=x[:], in0=x[:], scalar=gfin[:, c:c + 1], in1=C.rstd[:],
                                                                        op0=ALU.mult, op1=ALU.mult),
                     reads=[x, gfin, C.rstd], writes=[x])
    if nxt is not None:
        C.store_x(xo_d)
    else:
        C.store_x(yo_d)
    maps = []
    for i in range(NCORE):
        m = {"x": xT_sh[i], "mix": mixT_sh[i], "wmo": w_mo, "gxa": vec128(g_xa), "gmem": vec128(g_mem), "mem": memT,
             "wq": wq, "wkv": wkv, "wo": wo, "gf2": vec128(g_f2), "w2i": w2_in, "w2o": w2_out}
        if nxt is not None:
            m.update({"gn1": vec128(nxt[0]), "wn1i": nxt[1], "wn1o": nxt[2], "gnm": vec128(nxt[3])})
        elif not plain:
            m["gfin"] = vec128(g_final)
        maps.append(m)
    res = run_prog(P, maps)
    if nxt is not None:
        return [r["xo"] for r in res], [r["ho"] for r in res]
    return [r["yo"] for r in res]


def head_proj(h_allT, w_cores, M):
    Mp = ((M + 127) // 128) * 128
    if Mp != M:
        w_cores = [np.ascontiguousarray(np.concatenate([w, np.zeros((D, Mp - M), np.float32)], axis=1)) for w in w_cores]
        M = Mp
    P = Prog()
    h_d = P.dram("h", [D, S], BF16, kind="ExternalInput")
    w_d = P.dram("w", [D, M], F32, kind="ExternalInput")
    o_d = P.dram("o", [M, S], F32, kind="ExternalOutput")
    W = P.sbuf([128, 8, M], BF16, name="Wp")
    wv = w_d.t.ap().rearrange("(kc p) n -> p kc n", p=128)
    P.dma("pool", W[:], wv, reads=[w_d], writes=[W])
    hv = h_d.t.ap().rearrange("(kc p) n -> p kc n", p=128)
    hb = Ring([P.sbuf([128, 8, 512], BF16, name=f"hb{i}") for i in range(3)])
    pss = Ring([P.psum([128, 512], F32, name=f"pp{i}") for i in range(4)])
    stg = Ring([P.sbuf([128, 512], F32, name=f"stg{i}") for i in range(4)])
    groups = [(m0, min(m0 + 128, M)) for m0 in range(0, M, 128)]
    k = 0
    for t in range(S // 512):
        h = hb.next()
        P.dma("sp", h[:], hv[:, :, t * 512:(t + 1) * 512], reads=[h_d], writes=[h])
        for (m0, m1) in groups:
            ps = pss.next()
            m = m1 - m0
            for kc in range(8):
                P.op("pe", lambda e, ps=ps, h=h, kc=kc, m0=m0, m1=m1, m=m: e.matmul(ps[0:m, :], lhsT=W[:, kc, m0:m1], rhs=h[:, kc, :],
                                                                                     start=(kc == 0), stop=(kc == 7)),
                     reads=[W, h], writes=[ps])
            sg = stg.next()
            eng = "act" if k % 2 == 0 else "dve"
            k += 1
            if eng == "act":
                P.op("act", lambda e, ps=ps, sg=sg, m=m: e.copy(out=sg[0:m, :], in_=ps[0:m, :]), reads=[ps], writes=[sg])
            else:
                P.op("dve", lambda e, ps=ps, sg=sg, m=m: e.tensor_copy(out=sg[0:m, :], in_=ps[0:m, :]), reads=[ps], writes=[sg])
            P.dma("sp", o_d[m0:m1, t * 512:(t + 1) * 512], sg[0:m, :], reads=[sg], writes=[o_d])
    res = run_prog(P, [{"h": h_allT, "w": w_cores[i]} for i in range(NCORE)])
    return [r["o"] for r in res]


def banded_attn(q3, k3, v3, maskT):
    P = Prog()
    q_d = P.dram("q", [3, 64, S], F32, kind="ExternalInput")
    k_d = P.dram("k", [3, 64, S], F32, kind="ExternalInput")
    v_d = P.dram("v", [3, S, 64], F32, kind="ExternalInput")
    m_d = P.dram("m", [128, 256], F32, kind="ExternalInput")
    o_d = P.dram("o", [3, 65, S], F32, kind="ExternalOutput")
    qT = P.sbuf([64, S], BF16, name="qT")
    kT = P.sbuf([64, S], BF16, name="kT")
    Vx = P.sbuf([128, 128, 65], BF16, name="Vx")
    mk = P.sbuf([128, 256], F32, name="mk")
    P.dma("sp", mk[:], m_d[:], reads=[m_d], writes=[mk])
    Oacc = [P.sbuf([65, 2048], F32, name=f"Oacc{i}") for i in range(8)]
    sps = Ring([P.psum([128, 256], F32, name=f"sps{i}") for i in range(4)])
    ops_ = Ring([P.psum([128, 128], F32, name=f"ops{i}") for i in range(4)])
    ssb = Ring([P.sbuf([128, 256], F32, name=f"ssb{i}") for i in range(4)])
    pTr = Ring([P.sbuf([128, 256], BF16, name=f"pTr{i}") for i in range(4)])
    for pi, dil in enumerate((1, 4, 16)):
        nseg = 128 // dil
        for a in range(0, S, 2048):
            P.dma("pool", qT[:, a:a + 2048], q_d[pi, :, a:a + 2048], reads=[q_d], writes=[qT])
            P.dma("pool", kT[:, a:a + 2048], k_d[pi, :, a:a + 2048], reads=[k_d], writes=[kT])
        vv = v_d.t.ap()[pi].rearrange("(b p) f -> p b f", p=128)
        for b0 in range(0, 128, 32):
            P.dma("pool", Vx[:, b0:b0 + 32, 0:64], vv[:, b0:b0 + 32, :], reads=[v_d], writes=[Vx])
        P.op("dve", lambda e: e.memset(Vx[:, :, 64:65], 1.0), writes=[Vx])
        for b in range(128):
            first = (b % nseg == 0)
            c0 = 128 if first else 0
            sp_ = sps.next()
            qs = slice(b * 128, (b + 1) * 128)
            if not first:
                P.op("pe", lambda e, sp_=sp_, b=b, qs=qs: e.matmul(sp_[:, 0:128], lhsT=kT[:, (b - 1) * 128:b * 128], rhs=qT[:, qs], start=True, stop=True),
                     reads=[kT, qT], writes=[sp_])
            P.op("pe", lambda e, sp_=sp_, qs=qs: e.matmul(sp_[:, 128:256], lhsT=kT[:, qs], rhs=qT[:, qs], start=True, stop=True),
                 reads=[kT, qT], writes=[sp_])
            sb = ssb.next()
            P.op("dve", lambda e, sb=sb, sp_=sp_, c0=c0: e.tensor_tensor(out=sb[:, c0:256], in0=sp_[:, c0:256], in1=mk[:, c0:256], op=ALU.add),
                 reads=[sp_, mk], writes=[sb])
            pT = pTr.next()
            P.op("act", lambda e, sb=sb, pT=pT, c0=c0: e.activation(out=pT[:, c0:256], in_=sb[:, c0:256], func=AF.Exp, scale=0.125),
                 reads=[sb], writes=[pT])
            op_ = ops_.next()
            if not first:
                P.op("pe", lambda e, op_=op_, pT=pT, b=b: e.matmul(op_[0:65, :], lhsT=Vx[:, b - 1, :], rhs=pT[:, 0:128], start=True, stop=False),
                     reads=[Vx, pT], writes=[op_])
            P.op("pe", lambda e, op_=op_, pT=pT, b=b, first=first: e.matmul(op_[0:65, :], lhsT=Vx[:, b, :], rhs=pT[:, 128:256], start=first, stop=True),
                 reads=[Vx, pT], writes=[op_])
            oa = Oacc[b // 16]
            P.op("act", lambda e, oa=oa, op_=op_, b=b: e.copy(out=oa[:, (b % 16) * 128:(b % 16 + 1) * 128], in_=op_[0:65, :]),
                 reads=[op_], writes=[oa])
            if b % 16 == 15:
                P.dma("sp", o_d[pi, :, (b // 16) * 2048:(b // 16 + 1) * 2048], oa[:], reads=[oa], writes=[o_d])
    res = run_prog(P, [{"q": q3[i], "k": k3[i], "v": v3[i], "m": maskT} for i in range(NCORE)])
    return [r["o"] for r in res]


def lru_combine(xr_pad, gr, cw, cb, wa, ba, wx, bx, lam, num3, den3):
    P = Prog()
    dI = lambda n, shp, dt=F32: P.dram(n, shp, dt, kind="ExternalInput")
    xr_d = dI("xr", [64, S + 3]); gr_d = dI("gr", [64, S]); cw_d = dI("cw", [64, 4]); cb_d = dI("cb", [64, 1])
    wa_d = dI("wa", [64, 64]); ba_d = dI("ba", [64, 1]); wx_d = dI("wx", [64, 64]); bx_d = dI("bx", [64, 1]); lam_d = dI("lam", [64, 1])
    n_d = dI("num", [3, 64, S]); d_d = dI("den", [3, 64, S])
    at_d = P.dram("attn", [64, S], BF16, kind="ExternalOutput")
    y_d = P.dram("y", [64, S], BF16, kind="ExternalOutput")

    def small(dr, shp, nm, q="sp", dt=F32):
        b = P.sbuf(shp, dt, name=nm)
        P.dma(q, b[:], dr[:], reads=[dr], writes=[b])
        return b
    cws = small(cw_d, [64, 4], "cws"); cbs = small(cb_d, [64, 1], "cbs"); bas = small(ba_d, [64, 1], "bas"); bxs = small(bx_d, [64, 1], "bxs")
    lams = small(lam_d, [64, 1], "lams")
    was = small(wa_d, [64, 64], "was", q="pool", dt=BF16); wxs = small(wx_d, [64, 64], "wxs", q="pool", dt=BF16)
    one1 = P.sbuf([64, 1], F32, name="one1")
    P.op("dve", lambda e: e.memset(one1[:], 1.0), writes=[one1])
    c8 = P.sbuf([64, 1], F32, name="c8")
    P.op("act", lambda e: e.activation(out=c8[:], in_=lams[:], func=AF.Exp, scale=-1.0), reads=[lams], writes=[c8])
    P.op("act", lambda e: e.activation(out=c8[:], in_=c8[:], func=AF.Ln, bias=one1[:, 0:1], scale=1.0), reads=[c8, one1], writes=[c8])
    P.op("dve", lambda e: e.tensor_scalar(out=c8[:], in0=c8[:], scalar1=-8.0, scalar2=None, op0=ALU.mult), reads=[c8], writes=[c8])
    CH = 2048
    mk = lambda nm, n=2, w=CH, dt=F32: Ring([P.sbuf([64, w], dt, name=f"{nm}{i}") for i in range(n)])
    xr_r = mk("xr", 2, CH + 3); gr_r = mk("gr"); xc_r = mk("xc", 1); xcb_r = mk("xcb", 1, CH, BF16)
    r_r = mk("r", 1); i_r = mk("i", 1); a_r = mk("a", 1); t_r = mk("t", 1); hs_r = mk("hs"); u_r = mk("u", 1)
    yb_r = mk("yb", 2, CH, BF16); ab_r = mk("ab", 2, CH, BF16)
    n_r = [mk(f"n{j}", 1) for j in range(3)]; dn_r = [mk(f"dn{j}", 1) for j in range(3)]
    psr = Ring([P.psum([64, 512], F32, name=f"psr{i}") for i in range(2)])
    psi = Ring([P.psum([64, 512], F32, name=f"psi{i}") for i in range(2)])
    hs_prev = None
    for ch in range(S // CH):
        c0 = ch * CH
        xr = xr_r.next(); gr_ = gr_r.next(); xc = xc_r.next(); xcb = xcb_r.next()
        P.dma("sp", xr[:], xr_d[:, c0:c0 + CH + 3], reads=[xr_d], writes=[xr])
        P.dma("sp", gr_[:], gr_d[:, c0:c0 + CH], reads=[gr_d], writes=[gr_])
        P.op("dve", lambda e, xc=xc, xr=xr: e.tensor_scalar(out=xc[:], in0=xr[:, 3:CH + 3], scalar1=cws[:, 3:4], scalar2=cbs[:, 0:1], op0=ALU.mult, op1=ALU.add),
             reads=[xr, cws, cbs], writes=[xc])
        for k in range(3):
            P.op("dve", lambda e, xc=xc, xr=xr, k=k: e.scalar_tensor_tensor(out=xc[:], in0=xr[:, k:k + CH], scalar=cws[:, k:k + 1], in1=xc[:], op0=ALU.mult, op1=ALU.add),
                 reads=[xr, cws, xc], writes=[xc])
        P.op("act", lambda e, xc=xc, xcb=xcb: e.copy(out=xcb[:], in_=xc[:]), reads=[xc], writes=[xcb])
        r = r_r.next(); ii = i_r.next()
        for sub in range(CH // 512):
            ss = slice(sub * 512, (sub + 1) * 512)
            pr = psr.next(); pi_ = psi.next()
            P.op("pe", lambda e, pr=pr, xcb=xcb, ss=ss: e.matmul(pr[:], lhsT=was[:], rhs=xcb[:, ss], start=True, stop=True), reads=[was, xcb], writes=[pr])
            P.op("pe", lambda e, pi_=pi_, xcb=xcb, ss=ss: e.matmul(pi_[:], lhsT=wxs[:], rhs=xcb[:, ss], start=True, stop=True), reads=[wxs, xcb], writes=[pi_])
            P.op("act", lambda e, pr=pr, r=r, ss=ss: e.activation(out=r[:, ss], in_=pr[:], func=AF.Sigmoid, bias=bas[:, 0:1], scale=1.0), reads=[pr, bas], writes=[r])
            P.op("act", lambda e, pi_=pi_, ii=ii, ss=ss: e.activation(out=ii[:, ss], in_=pi_[:], func=AF.Sigmoid, bias=bxs[:, 0:1], scale=1.0), reads=[pi_, bxs], writes=[ii])
        a = a_r.next(); tt = t_r.next()
        P.op("act", lambda e, a=a, r=r: e.activation(out=a[:], in_=r[:], func=AF.Exp, scale=c8[:, 0:1]), reads=[r, c8], writes=[a])
        P.op("pool", lambda e, a=a, tt=tt: e.tensor_tensor(out=tt[:], in0=a[:], in1=a[:], op=ALU.mult), reads=[a], writes=[tt])
        P.op("act", lambda e, tt=tt: e.activation(out=tt[:], in_=tt[:], func=AF.Sqrt, bias=one1[:, 0:1], scale=-1.0), reads=[tt, one1], writes=[tt])
        P.op("pool", lambda e, tt=tt, ii=ii: e.tensor_tensor(out=tt[:], in0=tt[:], in1=ii[:], op=ALU.mult), reads=[tt, ii], writes=[tt])
        P.op("dve", lambda e, tt=tt, xc=xc: e.tensor_tensor(out=tt[:], in0=tt[:], in1=xc[:], op=ALU.mult), reads=[tt, xc], writes=[tt])
        hs = hs_r.next()
        if hs_prev is None:
            P.op("dve", lambda e, hs=hs, a=a, tt=tt: e.tensor_tensor_scan(out=hs[:], data0=a[:], data1=tt[:], initial=0.0, op0=ALU.mult, op1=ALU.add),
                 reads=[a, tt], writes=[hs])
        else:
            P.op("dve", lambda e, hs=hs, a=a, tt=tt, hp=hs_prev: e.tensor_tensor_scan(out=hs[:], data0=a[:], data1=tt[:], initial=hp[:, CH - 1:CH], op0=ALU.mult, op1=ALU.add),
                 reads=[a, tt, hs_prev], writes=[hs])
        hs_prev = hs
        u = u_r.next()
        P.op("pool", lambda e, u=u, g=gr_: e.tensor_tensor(out=u[:], in0=g[:], in1=g[:], op=ALU.mult), reads=[gr_], writes=[u])
        P.op("dve", lambda e, u=u: e.tensor_scalar(out=u[:], in0=u[:], scalar1=0.044715, scalar2=1.0, op0=ALU.mult, op1=ALU.add), reads=[u], writes=[u])
        P.op("pool", lambda e, u=u, g=gr_: e.tensor_tensor(out=u[:], in0=u[:], in1=g[:], op=ALU.mult), reads=[u, gr_], writes=[u])
        P.op("act", lambda e, u=u: e.activation(out=u[:], in_=u[:], func=AF.Sigmoid, scale=1.5957691216057308), reads=[u], writes=[u])
        P.op("pool", lambda e, u=u, g=gr_: e.tensor_tensor(out=u[:], in0=u[:], in1=g[:], op=ALU.mult), reads=[u, gr_], writes=[u])
        yb = yb_r.next()
        P.op("dve", lambda e, u=u, hs=hs, yb=yb: e.tensor_tensor(out=yb[:], in0=u[:], in1=hs[:], op=ALU.mult), reads=[u, hs], writes=[yb])
        P.dma("sp", y_d[:, c0:c0 + CH], yb[:], reads=[yb], writes=[y_d])
        ns = [n_r[j].next() for j in range(3)]; ds = [dn_r[j].next() for j in range(3)]
        for j in range(3):
            P.dma("sp", ns[j][:], n_d[j, :, c0:c0 + CH], reads=[n_d], writes=[ns[j]])
            P.dma("sp", ds[j][:], d_d[j, :, c0:c0 + CH], reads=[d_d], writes=[ds[j]])
        P.op("pool", lambda e, ns=ns: e.tensor_tensor(out=ns[0][:], in0=ns[0][:], in1=ns[1][:], op=ALU.add), reads=[ns[0], ns[1]], writes=[ns[0]])
        P.op("pool", lambda e, ns=ns: e.tensor_tensor(out=ns[0][:], in0=ns[0][:], in1=ns[2][:], op=ALU.add), reads=[ns[0], ns[2]], writes=[ns[0]])
        P.op("pool", lambda e, ds=ds: e.tensor_tensor(out=ds[0][:], in0=ds[0][:], in1=ds[1][:], op=ALU.add), reads=[ds[0], ds[1]], writes=[ds[0]])
        P.op("dve", lambda e, ds=ds: e.tensor_tensor(out=ds[0][:], in0=ds[0][:], in1=ds[2][:], op=ALU.add), reads=[ds[0], ds[2]], writes=[ds[0]])
        P.op("dve", lambda e, ds=ds: e.reciprocal(out=ds[0][:], in_=ds[0][:]), reads=[ds[0]], writes=[ds[0]])
        ab = ab_r.next()
        P.op("dve", lambda e, ab=ab, ns=ns, ds=ds: e.tensor_tensor(out=ab[:], in0=ns[0][:], in1=ds[0][:], op=ALU.mult), reads=[ns[0], ds[0]], writes=[ab])
        P.dma("sp", at_d[:, c0:c0 + CH], ab[:], reads=[ab], writes=[at_d])
    maps = [{"xr": xr_pad[i], "gr": gr[i], "cw": cw[i], "cb": cb[i], "wa": wa[i], "ba": ba[i], "wx": wx[i], "bx": bx[i], "lam": lam[i],
             "num": num3[i], "den": den3[i]} for i in range(NCORE)]
    res = run_prog(P, maps)
    return [r["attn"] for r in res], [r["y"] for r in res]


def layer0_mixer(h_allT, inp):
    c = np.ascontiguousarray
    w_in = inp["ab_w_in"][0]
    wc = [c(np.concatenate([w_in[:, o + 64 * i:o + 64 * (i + 1)] for o in (0, 512, 1024, 1536, 2048)], axis=1)) for i in range(NCORE)]
    proj = head_proj(h_allT, wc, 320)
    perms = [np.arange(S).reshape(S // d, d).T.reshape(-1) for d in (1, 4, 16)]
    q3 = [c(np.stack([p[0:64][:, pm] for pm in perms])) for p in proj]
    k3 = [c(np.stack([p[64:128][:, pm] for pm in perms])) for p in proj]
    v3 = [c(np.stack([p[128:192][:, pm].T for pm in perms])) for p in proj]
    pp = np.arange(128)[:, None]; qq = np.arange(128)[None, :]
    maskT = np.concatenate([np.where(pp >= qq, 0.0, NEG), np.where(pp <= qq, 0.0, NEG)], axis=1).astype(np.float32)
    O = banded_attn(q3, k3, v3, c(maskT))
    inv = [np.argsort(pm) for pm in perms]
    num3 = [c(np.stack([o[j, 0:64][:, inv[j]] for j in range(3)])) for o in O]
    den3 = [c(np.stack([np.broadcast_to(o[j, 64:65][:, inv[j]], (64, S)) for j in range(3)])) for o in O]
    xr_pad = [c(np.concatenate([np.zeros((64, 3), np.float32), p[192:256]], axis=1)) for p in proj]
    gr = [c(p[256:320]) for p in proj]
    sl = lambda v, i: c(np.asarray(v)[64 * i:64 * (i + 1)].reshape(64, 1))
    cw = [c(inp["lru_conv_w"][0][:, 64 * i:64 * (i + 1)].T) for i in range(NCORE)]
    cb = [sl(inp["lru_conv_b"][0], i) for i in range(NCORE)]
    wa = [c(inp["lru_w_a"][0][i]) for i in range(NCORE)]
    wx = [c(inp["lru_w_x"][0][i]) for i in range(NCORE)]
    ba = [sl(inp["lru_b_a"][0], i) for i in range(NCORE)]
    bx = [sl(inp["lru_b_x"][0], i) for i in range(NCORE)]
    lam = [sl(inp["lru_lambda"][0], i) for i in range(NCORE)]
    attn, y = lru_combine(xr_pad, gr, cw, cb, wa, ba, wx, bx, lam, num3, den3)
    mixT = np.concatenate(attn + y, axis=0)
    return mixT


def gdn_prep(qkv_pad, cw3, a128, b128, scal):
    P = Prog()
    dI = lambda n, shp, dt=F32: P.dram(n, shp, dt, kind="ExternalInput")
    x_d = dI("x", [3, 128, S + 3]); cw_d = dI("cw", [128, 12]); a_d = dI("a", [128, 128]); b_d = dI("b", [128, 128]); sc_d = dI("sc", [128, 2])
    o_d = P.dram("o", [3, 128, S], F32, kind="ExternalOutput")
    gc_d = P.dram("gc", [128, 128], F32, kind="ExternalOutput")
    be_d = P.dram("be", [128, 128], F32, kind="ExternalOutput")
    cws = P.sbuf([128, 12], F32, name="cws"); P.dma("sp", cws[:], cw_d[:], reads=[cw_d], writes=[cws])
    sc = P.sbuf([128, 2], F32, name="scs"); P.dma("sp", sc[:], sc_d[:], reads=[sc_d], writes=[sc])
    one1 = P.sbuf([128, 1], F32, name="one1"); P.op("dve", lambda e: e.memset(one1[:], 1.0), writes=[one1])
    eps1 = P.sbuf([128, 1], F32, name="eps1"); P.op("dve", lambda e: e.memset(eps1[:], 1e-6), writes=[eps1])
    ones = P.sbuf([128, 128], BF16, name="ones"); P.op("dve", lambda e: e.memset(ones[:], 1.0), writes=[ones])
    av = P.sbuf([128, 128], F32, name="av"); bv = P.sbuf([128, 128], F32, name="bv"); rm = P.sbuf([128, 128], F32, name="rm")
    gcv = P.sbuf([128, 128], F32, name="gcv"); nea = P.sbuf([128, 1], F32, name="nea")
    P.dma("sp", av[:], a_d[:], reads=[a_d], writes=[av]); P.dma("sp", bv[:], b_d[:], reads=[b_d], writes=[bv])
    P.op("act", lambda e: e.activation(out=nea[:], in_=sc[:, 0:1], func=AF.Exp), reads=[sc], writes=[nea])
    P.op("dve", lambda e: e.tensor_scalar(out=nea[:], in0=nea[:], scalar1=-1.0, scalar2=None, op0=ALU.mult), reads=[nea], writes=[nea])
    P.op("act", lambda e: e.activation(out=av[:], in_=av[:], func=AF.Exp, bias=sc[:, 1:2], scale=1.0), reads=[av, sc], writes=[av])
    P.op("act", lambda e: e.activation(out=av[:], in_=av[:], func=AF.Ln, bias=one1[:, 0:1], scale=1.0), reads=[av, one1], writes=[av])
    P.op("dve", lambda e: e.tensor_scalar(out=av[:], in0=av[:], scalar1=nea[:, 0:1], scalar2=None, op0=ALU.mult), reads=[av, nea], writes=[av])
    P.op("act", lambda e: e.activation(out=bv[:], in_=bv[:], func=AF.Sigmoid), reads=[bv], writes=[bv])
    P.op("dve", lambda e: e.memset(rm[:], 1.0), writes=[rm])
    P.op("dve", lambda e: e.memset(rm[:, 0:128:64], 0.0), writes=[rm])
    P.op("dve", lambda e: e.tensor_tensor_scan(out=gcv[:], data0=rm[:], data1=av[:], initial=0.0, op0=ALU.mult, op1=ALU.add), reads=[rm, av], writes=[gcv])
    P.dma("sp", gc_d[:], gcv[:], reads=[gcv], writes=[gc_d]); P.dma("sp", be_d[:], bv[:], reads=[bv], writes=[be_d])
    xin = Ring([P.sbuf([128, 515], F32, name=f"xin{i}") for i in range(6)])
    acc = Ring([P.sbuf([128, 512], F32, name=f"acc{i}") for i in range(6)])
    sqr = Ring([P.sbuf([128, 512], BF16, name=f"sqr{i}") for i in range(4)])
    rnr = Ring([P.sbuf([128, 512], F32, name=f"rnr{i}") for i in range(4)])
    pss = Ring([P.psum([128, 512], F32, name=f"pss{i}") for i in range(4)])
    for t in range(S // 512):
        xs = [xin.next() for w in range(3)]
        acs = [acc.next() for w in range(3)]
        for w in range(3):
            P.dma("sp", xs[w][:], x_d[w, :, t * 512:t * 512 + 515], reads=[x_d], writes=[xs[w]])
        for w in range(3):
            P.op("dve", lambda e, ac=acs[w], xi=xs[w], w=w: e.tensor_scalar(out=ac[:], in0=xi[:, 3:515], scalar1=cws[:, 4 * w + 3:4 * w + 4], scalar2=None, op0=ALU.mult),
                 reads=[xs[w], cws], writes=[acs[w]])
        for k in range(3):
            for w in range(3):
                P.op("dve", lambda e, ac=acs[w], xi=xs[w], w=w, k=k: e.scalar_tensor_tensor(out=ac[:], in0=xi[:, k:k + 512], scalar=cws[:, 4 * w + k:4 * w + k + 1], in1=ac[:], op0=ALU.mult, op1=ALU.add),
                     reads=[xs[w], cws, acs[w]], writes=[acs[w]])
        for w in range(3):
            P.op("act", lambda e, ac=acs[w]: e.activation(out=ac[:], in_=ac[:], func=AF.Silu), reads=[acs[w]], writes=[acs[w]])
        sqs = [sqr.next() for w in range(2)]; pps = [pss.next() for w in range(2)]; rns = [rnr.next() for w in range(2)]
        for w in range(2):
            P.op("act", lambda e, ac=acs[w], sq=sqs[w]: e.activation(out=sq[:], in_=ac[:], func=AF.Square), reads=[acs[w]], writes=[sqs[w]])
        for w in range(2):
            P.op("pe", lambda e, ps=pps[w], sq=sqs[w]: e.matmul(ps[:], lhsT=ones[:], rhs=sq[:], start=True, stop=True), reads=[ones, sqs[w]], writes=[pps[w]])
        for w in range(2):
            P.op("act", lambda e, ps=pps[w], rn=rns[w]: e.activation(out=rn[:], in_=ps[:], func=AF.Sqrt, bias=eps1[:, 0:1], scale=1.0), reads=[pps[w], eps1], writes=[rns[w]])
        for w in range(2):
            P.op("dve", lambda e, rn=rns[w]: e.reciprocal(out=rn[:], in_=rn[:]), reads=[rns[w]], writes=[rns[w]])
        for w in range(2):
            sc_ = (128.0 ** -0.5) if w == 0 else 1.0
            P.op("dve", lambda e, ac=acs[w], rn=rns[w], sc_=sc_: e.scalar_tensor_tensor(out=ac[:], in0=ac[:], scalar=sc_, in1=rn[:], op0=ALU.mult, op1=ALU.mult),
                 reads=[acs[w], rns[w]], writes=[acs[w]])
        for w in range(3):
            P.dma("sp", o_d[w, :, t * 512:(t + 1) * 512], acs[w][:], reads=[acs[w]], writes=[o_d])
    maps = [{"x": qkv_pad[i], "cw": cw3[i], "a": a128[i], "b": b128[i], "sc": scal[i]} for i in range(NCORE)]
    res = run_prog(P, maps)
    return [r["o"] for r in res], [r["gc"] for r in res], [r["be"] for r in res]


def gdn_amat(qk2, gcr, ber, gcc, mstrict, mincl):
    P = Prog()
    dI = lambda n, shp, dt=F32: P.dram(n, shp, dt, kind="ExternalInput")
    x_d = dI("x", [2, 128, S]); gcr_d = dI("gcr", [64, S]); ber_d = dI("ber", [64, S]); gcc_d = dI("gcc", [64, 256])
    ms_d = dI("ms", [64, 64]); mi_d = dI("mi", [64, 64])
    at_d = P.dram("at", [64, 256, 64], F32, kind="ExternalOutput")
    qk_d = P.dram("qk", [64, 256, 64], F32, kind="ExternalOutput")
    gcc_s = P.sbuf([64, 256], F32, name="gccs"); P.dma("sp", gcc_s[:], gcc_d[:], reads=[gcc_d], writes=[gcc_s])
    ms = P.sbuf([64, 64], F32, name="mss"); P.dma("sp", ms[:], ms_d[:], reads=[ms_d], writes=[ms])
    mi = P.sbuf([64, 64], F32, name="mis"); P.dma("sp", mi[:], mi_d[:], reads=[mi_d], writes=[mi])
    R = lambda nm, shp, dt=F32, n=2: Ring([P.sbuf(shp, dt, name=f"{nm}{i}") for i in range(n)])
    qb_r = R("qb", [128, 512], BF16); kb_r = R("kb", [128, 512], BF16); gr_r = R("gr", [64, 8, 64]); br_r = R("br", [64, 8, 64])
    dd_r = R("dd", [64, 8, 64]); es_r = R("es", [64, 8, 64]); ei_r = R("ei", [64, 8, 64]); at_r = R("ato", [64, 8, 64]); qo_r = R("qo", [64, 8, 64])
    pk_r = Ring([P.psum([64, 8, 64], F32, name=f"pk{i}") for i in range(2)])
    pq_r = Ring([P.psum([64, 8, 64], F32, name=f"pq{i}") for i in range(2)])
    for t in range(S // 512):
        n0 = t * 8
        ts_ = slice(t * 512, (t + 1) * 512)
        qb = qb_r.next(); kb = kb_r.next(); gr_ = gr_r.next(); br = br_r.next()
        P.dma("pool", qb[:], x_d[0, :, ts_], reads=[x_d], writes=[qb])
        P.dma("pool", kb[:], x_d[1, :, ts_], reads=[x_d], writes=[kb])
        P.dma("sp", gr_[:], gcr_d[:, ts_].rearrange("p (n i) -> p n i", i=64), reads=[gcr_d], writes=[gr_])
        P.dma("sp", br[:], ber_d[:, ts_].rearrange("p (n i) -> p n i", i=64), reads=[ber_d], writes=[br])
        pk = pk_r.next(); pq = pq_r.next()
        for n in range(8):
            cs = slice(n * 64, (n + 1) * 64)
            P.op("pe", lambda e, pk=pk, kb=kb, n=n, cs=cs: e.matmul(pk[:, n, :], lhsT=kb[:, cs], rhs=kb[:, cs], start=True, stop=True), reads=[kb], writes=[pk])
            P.op("pe", lambda e, pq=pq, kb=kb, qb=qb, n=n, cs=cs: e.matmul(pq[:, n, :], lhsT=kb[:, cs], rhs=qb[:, cs], start=True, stop=True), reads=[kb, qb], writes=[pq])
        dd = dd_r.next(); es = es_r.next(); ei = ei_r.next(); ato = at_r.next(); qo = qo_r.next()
        P.op("dve", lambda e, dd=dd, gr_=gr_, n0=n0: e.tensor_tensor(out=dd[:], in0=gr_[:], in1=gcc_s[:, n0:n0 + 8].unsqueeze(2).broadcast_to([64, 8, 64]), op=ALU.subtract),
             reads=[gr_, gcc_s], writes=[dd])
        P.op("dve", lambda e, dd=dd, es=es: e.tensor_tensor(out=es[:], in0=dd[:], in1=ms[:].unsqueeze(1).broadcast_to([64, 8, 64]), op=ALU.add), reads=[dd, ms], writes=[es])
        P.op("pool", lambda e, dd=dd, ei=ei: e.tensor_tensor(out=ei[:], in0=dd[:], in1=mi[:].unsqueeze(1).broadcast_to([64, 8, 64]), op=ALU.add), reads=[dd, mi], writes=[ei])
        P.op("act", lambda e, es=es: e.activation(out=es[:], in_=es[:], func=AF.Exp), reads=[es], writes=[es])
        P.op("act", lambda e, ei=ei: e.activation(out=ei[:], in_=ei[:], func=AF.Exp), reads=[ei], writes=[ei])
        P.op("pool", lambda e, es=es, br=br: e.tensor_tensor(out=es[:], in0=es[:], in1=br[:], op=ALU.mult), reads=[es, br], writes=[es])
        P.op("dve", lambda e, ato=ato, pk=pk, es=es: e.tensor_tensor(out=ato[:], in0=pk[:], in1=es[:], op=ALU.mult), reads=[pk, es], writes=[ato])
        P.op("dve", lambda e, qo=qo, pq=pq, ei=ei: e.tensor_tensor(out=qo[:], in0=pq[:], in1=ei[:], op=ALU.mult), reads=[pq, ei], writes=[qo])
        P.dma("sp", at_d[:, n0:n0 + 8, :], ato[:], reads=[ato], writes=[at_d])
        P.dma("sp", qk_d[:, n0:n0 + 8, :], qo[:], reads=[qo], writes=[qk_d])
    maps = [{"x": qk2[i], "gcr": gcr[i], "ber": ber[i], "gcc": gcc[i], "ms": mstrict, "mi": mincl} for i in range(NCORE)]
    res = run_prog(P, maps)
    return [r["at"] for r in res], [r["qk"] for r in res]


def gdn_solve(At2):
    P = Prog()
    a_d = P.dram("a", [2, 128, 4096], F32, kind="ExternalInput")
    t_d = P.dram("t", [2, 128, 4096], F32, kind="ExternalOutput")
    A = [P.sbuf([128, 64, 64], F32, name=f"A{g}") for g in range(2)]
    T = [P.sbuf([128, 64, 64], F32, name=f"T{g}") for g in range(2)]
    tmp = [P.sbuf([128, 4096], F32, name=f"tmp{g}") for g in range(2)]
    for g in range(2):
        P.dma("sp", A[g][:].rearrange("p a b -> p (a b)"), a_d[g], reads=[a_d], writes=[A[g]])
        P.op("dve", lambda e, g=g: e.memset(T[g][:], 0.0), writes=[T[g]])
        P.op("dve", lambda e, g=g: e.memset(T[g][:].rearrange("p a b -> p (a b)")[:, 0:4096:65], 1.0), writes=[T[g]])
    for i in range(1, 64):
        for g in range(2):
            tv = tmp[g][:, 0:i * i].rearrange("p (c j) -> p c j", j=i)
            P.op("dve", lambda e, g=g, i=i, tv=tv: e.tensor_tensor(out=tv, in0=T[g][:, 0:i, 0:i],
                                                                 in1=A[g][:, 0:i, i:i + 1].rearrange("p j o -> p o j").broadcast_to([128, i, i]), op=ALU.mult),
                 reads=[T[g], A[g]], writes=[tmp[g]])
            P.op("dve", lambda e, g=g, i=i, tv=tv: e.tensor_reduce(out=T[g][:, 0:i, i], in_=tv, axis=AX.X, op=ALU.add, negate=True),
                 reads=[tmp[g]], writes=[T[g]])
    for g in range(2):
        P.dma("sp", t_d[g], T[g][:].rearrange("p a b -> p (a b)"), reads=[T[g]], writes=[t_d])
    res = run_prog(P, [{"a": At2[i]} for i in range(NCORE)])
    return [r["t"] for r in res]


def gdn_scan(Tt, vtok, ktok, ztok, bec, gcc, glc, qT, gcr128, glb128, qkT, onb):
    P = Prog()
    dI = lambda n, shp, dt=F32: P.dram(n, shp, dt, kind="ExternalInput")
    T_d = dI("T", [64, 256, 64]); v_d = dI("v", [64, 256, 128]); k_d = dI("k", [64, 256, 128]); z_d = dI("z", [64, 256, 128])
    bec_d = dI("bec", [64, 256]); gcc_d = dI("gcc", [64, 256]); glc_d = dI("glc", [64, 256])
    q_d = dI("q", [128, S]); gcr_d = dI("gcr", [128, S]); glb_d = dI("glb", [128, 256]); qk_d = dI("qk", [64, 256, 64]); on_d = dI("on", [64, 128])
    o_d = P.dram("o", [64, 256, 128], BF16, kind="ExternalOutput")

    def small(dr, shp, nm):
        b = P.sbuf(shp, F32, name=nm)
        P.dma("sp", b[:], dr[:], reads=[dr], writes=[b])
        return b
    bec_s = small(bec_d, [64, 256], "becs"); gcc_s = small(gcc_d, [64, 256], "gccs"); glc_s = small(glc_d, [64, 256], "glcs")
    egl = small(glb_d, [128, 256], "egl"); onb_s = small(on_d, [64, 128], "onbs")
    cb = P.sbuf([64, 256], F32, name="cb"); cd = P.sbuf([64, 256], F32, name="cd")
    eps1 = P.sbuf([64, 1], F32, name="eps1"); P.op("dve", lambda e: e.memset(eps1[:], EPS), writes=[eps1])
    P.op("act", lambda e: e.activation(out=cb[:], in_=gcc_s[:], func=AF.Exp), reads=[gcc_s], writes=[cb])
    P.op("dve", lambda e: e.tensor_tensor(out=cb[:], in0=cb[:], in1=bec_s[:], op=ALU.mult), reads=[cb, bec_s], writes=[cb])
    P.op("dve", lambda e: e.tensor_tensor(out=cd[:], in0=glc_s[:], in1=gcc_s[:], op=ALU.subtract), reads=[glc_s, gcc_s], writes=[cd])
    P.op("act", lambda e: e.activation(out=cd[:], in_=cd[:], func=AF.Exp), reads=[cd], writes=[cd])
    P.op("act", lambda e: e.activation(out=egl[:], in_=egl[:], func=AF.Exp), reads=[egl], writes=[egl])
    S32 = [P.sbuf([128, 128], F32, name=f"S32{i}") for i in range(2)]; Sb = P.sbuf([128, 128], BF16, name="Sb")
    P.op("dve", lambda e: e.memset(S32[0][:], 0.0), writes=[S32[0]]); P.op("dve", lambda e: e.memset(S32[1][:], 0.0), writes=[S32[1]])
    P.op("dve", lambda e: e.memset(Sb[:], 0.0), writes=[Sb])
    nchunk = 0
    R = lambda nm, shp, dt=F32, n=2: Ring([P.sbuf(shp, dt, name=f"{nm}{i}") for i in range(n)])
    Tb_r = R("Tb", [64, 8, 64], BF16); vt_r = R("vt", [64, 8, 128]); kt_r = R("kt", [64, 8, 128]); zt_r = R("zt", [64, 8, 128])
    qkb_r = R("qkb", [64, 8, 64], BF16); qt_r = R("qt", [128, 512]); gt_r = R("gt", [128, 512])
    vb_r = R("vb", [64, 8, 128], BF16); kbe_r = R("kbe", [64, 8, 128], BF16); kd_r = R("kd", [64, 8, 128], BF16)
    qe_r = R("qe", [128, 512], BF16); u_r = R("u", [64, 8, 128]); wT_r = R("wT", [128, 8, 64], BF16); ot_r = R("ot", [64, 8, 128], BF16)
    vn_r = R("vn", [64, 128], BF16); jk_r = R("jk", [64, 128]); ss_r = R("ss", [64, 1], F32, 4)
    pu_r = Ring([P.psum([64, 128], F32, name=f"pu{i}") for i in range(2)])
    pw_r = Ring([P.psum([128, 64], F32, name=f"pw{i}") for i in range(1)])
    p1_r = Ring([P.psum([64, 128], F32, name=f"p1{i}") for i in range(2)])
    p2_r = Ring([P.psum([64, 128], F32, name=f"p2{i}") for i in range(2)])
    p3_r = Ring([P.psum([128, 128], F32, name=f"p3{i}") for i in range(1)])
    for t in range(S // 512):
        n0 = t * 8
        ns = slice(n0, n0 + 8)
        Tb = Tb_r.next(); vt = vt_r.next(); kt = kt_r.next(); zt = zt_r.next(); qkb = qkb_r.next(); qt = qt_r.next(); gt = gt_r.next()
        P.dma("pool", Tb[:], T_d[:, ns, :], reads=[T_d], writes=[Tb])
        P.dma("pool", qkb[:], qk_d[:, ns, :], reads=[qk_d], writes=[qkb])
        P.dma("sp", vt[:], v_d[:, ns, :], reads=[v_d], writes=[vt])
        P.dma("sp", kt[:], k_d[:, ns, :], reads=[k_d], writes=[kt])
        P.dma("sp", zt[:], z_d[:, ns, :], reads=[z_d], writes=[zt])
        P.dma("sp", qt[:], q_d[:, t * 512:(t + 1) * 512], reads=[q_d], writes=[qt])
        P.dma("sp", gt[:], gcr_d[:, t * 512:(t + 1) * 512], reads=[gcr_d], writes=[gt])
        vb = vb_r.next(); kbe = kbe_r.next(); kd = kd_r.next(); qe = qe_r.next()
        bc = lambda s_: s_[:, ns].unsqueeze(2).broadcast_to([64, 8, 128])
        b1, b2, b3 = bc(bec_s), bc(cb), bc(cd)
        P.op("pool", lambda e, vb=vb, vt=vt, b1=b1: e.tensor_tensor(out=vb[:], in0=vt[:], in1=b1, op=ALU.mult), reads=[vt, bec_s], writes=[vb])
        P.op("pool", lambda e, kbe=kbe, kt=kt, b2=b2: e.tensor_tensor(out=kbe[:], in0=kt[:], in1=b2, op=ALU.mult), reads=[kt, cb], writes=[kbe])
        P.op("pool", lambda e, kd=kd, kt=kt, b3=b3: e.tensor_tensor(out=kd[:], in0=kt[:], in1=b3, op=ALU.mult), reads=[kt, cd], writes=[kd])
        P.op("act", lambda e, gt=gt: e.activation(out=gt[:], in_=gt[:], func=AF.Exp), reads=[gt], writes=[gt])
        P.op("pool", lambda e, qe=qe, qt=qt, gt=gt: e.tensor_tensor(out=qe[:], in0=qt[:], in1=gt[:], op=ALU.mult), reads=[qt, gt], writes=[qe])
        P.op("act", lambda e, zt=zt: e.activation(out=zt[:], in_=zt[:], func=AF.Silu), reads=[zt], writes=[zt])
        P.op("pool", lambda e, zt=zt: e.tensor_tensor(out=zt[:], in0=zt[:], in1=onb_s[:].unsqueeze(1).broadcast_to([64, 8, 128]), op=ALU.mult), reads=[zt, onb_s], writes=[zt])
        u = u_r.next(); wT = wT_r.next(); ot = ot_r.next()
        for n in range(8):
            pu = pu_r.next(); pw = pw_r.next()
            P.op("pe", lambda e, pu=pu, Tb=Tb, vb=vb, n=n: e.matmul(pu[:], lhsT=Tb[:, n, :], rhs=vb[:, n, :], start=True, stop=True), reads=[Tb, vb], writes=[pu])
            P.op("pe", lambda e, pw=pw, Tb=Tb, kbe=kbe, n=n: e.matmul(pw[:], lhsT=kbe[:, n, :], rhs=Tb[:, n, :], start=True, stop=True), reads=[Tb, kbe], writes=[pw])
            P.op("act", lambda e, pu=pu, u=u, n=n: e.copy(out=u[:, n, :], in_=pu[:]), reads=[pu], writes=[u])
            P.op("act", lambda e, pw=pw, wT=wT, n=n: e.copy(out=wT[:, n, :], in_=pw[:]), reads=[pw], writes=[wT])
        for n in range(8):
            p1 = p1_r.next(); p2 = p2_r.next(); p3 = p3_r.next(); vn = vn_r.next()
            P.op("pe", lambda e, p1=p1, wT=wT, n=n: e.matmul(p1[:], lhsT=wT[:, n, :], rhs=Sb[:], start=True, stop=True), reads=[wT, Sb], writes=[p1])
            P.op("dve", lambda e, vn=vn, u=u, p1=p1, n=n: e.tensor_tensor(out=vn[:], in0=u[:, n, :], in1=p1[:], op=ALU.subtract), reads=[u, p1], writes=[vn])
            P.op("pe", lambda e, p2=p2, qe=qe, n=n: e.matmul(p2[:], lhsT=qe[:, n * 64:(n + 1) * 64], rhs=Sb[:], start=True, stop=False), reads=[qe, Sb], writes=[p2])
            P.op("pe", lambda e, p2=p2, qkb=qkb, vn=vn, n=n: e.matmul(p2[:], lhsT=qkb[:, n, :], rhs=vn[:], start=False, stop=True), reads=[qkb, vn], writes=[p2])
            P.op("pe", lambda e, p3=p3, kd=kd, vn=vn, n=n: e.matmul(p3[:], lhsT=kd[:, n, :], rhs=vn[:], start=True, stop=True), reads=[kd, vn], writes=[p3])
            cur = S32[nchunk % 2]; nxt_ = S32[(nchunk + 1) % 2]; nchunk += 1
            P.op("dve", lambda e, p3=p3, n0=n0, n=n, cur=cur: e.scalar_tensor_tensor(out=Sb[:], in0=cur[:], scalar=egl[:, n0 + n:n0 + n + 1], in1=p3[:], op0=ALU.mult, op1=ALU.add),
                 reads=[cur, egl, p3], writes=[Sb])
            P.op("dve", lambda e, p3=p3, n0=n0, n=n, cur=cur, nxt_=nxt_: e.scalar_tensor_tensor(out=nxt_[:], in0=cur[:], scalar=egl[:, n0 + n:n0 + n + 1], in1=p3[:], op0=ALU.mult, op1=ALU.add),
                 reads=[cur, egl, p3], writes=[nxt_])
            jk = jk_r.next(); ss = ss_r.next()
            P.op("act", lambda e, jk=jk, p2=p2, ss=ss: e.activation(out=jk[:], in_=p2[:], func=AF.Square, accum_out=ss[:, 0:1]), reads=[p2], writes=[jk, ss])
            P.op("act", lambda e, ss=ss: e.activation(out=ss[:], in_=ss[:], func=AF.Sqrt, bias=eps1[:, 0:1], scale=1.0 / 128), reads=[ss, eps1], writes=[ss])
            P.op("dve", lambda e, ss=ss: e.reciprocal(out=ss[:], in_=ss[:]), reads=[ss], writes=[ss])
            P.op("dve", lambda e, ot=ot, p2=p2, ss=ss, zt=zt, n=n: e.scalar_tensor_tensor(out=ot[:, n, :], in0=p2[:], scalar=ss[:, 0:1], in1=zt[:, n, :], op0=ALU.mult, op1=ALU.mult),
                 reads=[p2, ss, zt], writes=[ot])
        P.dma("sp", o_d[:, ns, :], ot[:], reads=[ot], writes=[o_d])
    maps = [{"T": Tt[i], "v": vtok[i], "k": ktok[i], "z": ztok[i], "bec": bec[i], "gcc": gcc[i], "glc": glc[i], "q": qT[i], "gcr": gcr128[i],
             "glb": glb128[i], "qk": qkT[i], "on": onb} for i in range(NCORE)]
    res = run_prog(P, maps)
    return [r["o"] for r in res]


def layer1_mixer(h_allT, inp):
    c = np.ascontiguousarray
    w_in = inp["dn_w_in"][0]
    wc = [c(np.concatenate([w_in[:, o + 128 * i:o + 128 * (i + 1)] for o in (0, 1024, 2048, 3072)] + [w_in[:, 4096 + i:4097 + i], w_in[:, 4104 + i:4105 + i]], axis=1))
          for i in range(NCORE)]
    proj = head_proj(h_allT, wc, 514)
    cwT = inp["dn_conv_w"][0].T
    qkv_pad, cw3, a128, b128, scal = [], [], [], [], []
    for i, p in enumerate(proj):
        x3 = p[0:384].reshape(3, 128, S)
        qkv_pad.append(c(np.concatenate([np.zeros((3, 128, 3), np.float32), x3], axis=2)))
        cw3.append(c(np.concatenate([cwT[o + 128 * i:o + 128 * (i + 1)] for o in (0, 1024, 2048)], axis=1)))
        a128.append(c(p[512].reshape(128, 128))); b128.append(c(p[513].reshape(128, 128)))
        scal.append(c(np.broadcast_to(np.array([inp["dn_a_log"][0][i], inp["dn_dt_bias"][0][i]], np.float32)[None, :], (128, 2))))
    qkv, gc, be = gdn_prep(qkv_pad, cw3, a128, b128, scal)
    gcrow = [g.reshape(S) for g in gc]; berow = [b.reshape(S) for b in be]
    col = lambda r: c(r.reshape(256, 64).T)
    gcc = [col(r) for r in gcrow]; bec = [col(r) for r in berow]
    glc = [c(np.broadcast_to(r.reshape(256, 64)[:, 63][None, :], (64, 256))) for r in gcrow]
    jj = np.arange(64)[:, None]; ii = np.arange(64)[None, :]
    mstrict = np.where(jj < ii, 0.0, NEG).astype(np.float32); mincl = np.where(jj <= ii, 0.0, NEG).astype(np.float32)
    At, qkT = gdn_amat([c(x[0:2]) for x in qkv], [c(np.broadcast_to(r[None, :], (64, S))) for r in gcrow],
                       [c(np.broadcast_to(r[None, :], (64, S))) for r in berow], gcc, mstrict, mincl)
    At2 = [c(a.transpose(1, 0, 2).reshape(2, 128, 4096)) for a in At]
    Tt = gdn_solve(At2)
    Ttj = [c(t_.reshape(256, 64, 64).transpose(1, 0, 2)) for t_ in Tt]
    tok = lambda xT: c(xT.reshape(128, 256, 64).transpose(2, 1, 0))
    vtok = [tok(x[2]) for x in qkv]; ktok = [tok(x[1]) for x in qkv]; ztok = [tok(p[384:512]) for p in proj]
    gcr128 = [c(np.broadcast_to(r[None, :], (128, S))) for r in gcrow]
    glb128 = [c(np.broadcast_to(r.reshape(256, 64)[:, 63][None, :], (128, 256))) for r in gcrow]
    onb = c(np.broadcast_to(inp["dn_o_norm"][0][None, :], (64, 128)).astype(np.float32))
    o = gdn_scan(Ttj, vtok, ktok, ztok, bec, gcc, glc, [c(x[0]) for x in qkv], gcr128, glb128, qkT, onb)
    oT = [c(x.transpose(1, 0, 2).reshape(S, 128).T) for x in o]
    return np.concatenate(oT, axis=0)


def phase_F(xT_sh, g_final):
    P = Prog()
    x_d = P.dram("x", [D, TPC], F32, kind="ExternalInput")
    g_d = P.dram("gfin", [128, 8], F32, kind="ExternalInput")
    yo_d = P.dram("yo", [D, TPC], F32, kind="ExternalOutput")
    C = TokCtx(P)
    C.eps_ap()
    gfin = C.load_small(g_d, [128, 8], "gfin")
    C.load_x(x_d)
    for t in range(TPC // 512):
        ps = C.ps_stat
        for c in range(8):
            sq = C.sq.next()
            x = C.xT[c][t]
            P.op("act", lambda e, sq=sq, x=x: e.activation(out=sq[:], in_=x[:], func=AF.Square), reads=[x], writes=[sq])
            P.op("pe", lambda e, sq=sq, c=c: e.matmul(ps[:], lhsT=C.ones[:], rhs=sq[:], start=(c == 0), stop=(c == 7)),
                 reads=[sq, C.ones], writes=[ps])
        P.op("act", lambda e: e.activation(out=C.rstd[:], in_=ps[:], func=AF.Sqrt, scale=1.0 / D, bias=C.eps_ap()),
             reads=[ps, C.epsb], writes=[C.rstd])
        P.op("dve", lambda e: e.reciprocal(out=C.rstd[:], in_=C.rstd[:]), reads=[C.rstd], writes=[C.rstd])
        for c in range(8):
            x = C.xT[c][t]
            P.op("dve", lambda e, c=c, x=x: e.scalar_tensor_tensor(out=x[:], in0=x[:], scalar=gfin[:, c:c + 1], in1=C.rstd[:],
                                                                    op0=ALU.mult, op1=ALU.mult),
                 reads=[x, gfin, C.rstd], writes=[x])
    C.store_x(yo_d)
    res = run_prog(P, [{"x": xT_sh[i], "gfin": vec128(g_final)} for i in range(NCORE)])
    return [r["yo"] for r in res]


def kernel(**inp):
    inp = {k: np.asarray(v) for k, v in inp.items()}
    c = np.ascontiguousarray
    sh = lambda xT: [c(xT[:, i * TPC:(i + 1) * TPC]) for i in range(NCORE)]
    xT = c(inp["x"][0].T)
    memT = c(inp["mem"][0].T)
    L = lambda n, l: c(inp[n][l])
    x1, h = phase_A(sh(xT), inp["ffn1_norm"][0], L("ffn1_w_in", 0), L("ffn1_w_out", 0), inp["mix_norm"][0])
    mix = layer0_mixer(c(np.concatenate(h, axis=1)), inp)
    x5, h = phase_C(x1, sh(mix), L("ab_w_out", 0), inp["xa_norm"][0], inp["xa_mem_norm"][0], memT, L("xa_wq", 0), L("xa_wkv", 0), L("xa_wo", 0),
                    inp["ffn2_norm"][0], L("ffn2_w_in", 0), L("ffn2_w_out", 0),
                    nxt=(inp["ffn1_norm"][1], L("ffn1_w_in", 1), L("ffn1_w_out", 1), inp["mix_norm"][1]))
    mix = layer1_mixer(c(np.concatenate(h, axis=1)), inp)
    y = phase_C(x5, sh(mix), L("dn_w_out", 0), inp["xa_norm"][1], inp["xa_mem_norm"][1], memT, L("xa_wq", 1), L("xa_wkv", 1), L("xa_wo", 1),
                inp["ffn2_norm"][1], L("ffn2_w_in", 1), L("ffn2_w_out", 1), g_final=inp["final_norm"])
    out = np.concatenate(y, axis=1).T
    return np.ascontiguousarray(out[None]).astype(np.float32)
```
